# Optimizing a Trainium2 kernel written in Bass

```python
import math
import jax, jax.numpy as jnp
from jax import lax
import numpy as np

D_MODEL = 2048
BATCH = 4
SEQ = 4096
DEPTH = 1

CHUNK = 64
N_META = 16
PAD_LEAD = CHUNK - N_META
EPS = 1e-6

SSD_HEADS = 32
SSD_HEAD_DIM = 64
D_SSD = SSD_HEADS * SSD_HEAD_DIM
SSD_GROUPS = 8
SSD_HEADS_PER_GROUP = SSD_HEADS // SSD_GROUPS
D_STATE = 128
CONV_WIDTH = 4
D_CONV = D_SSD + 2 * SSD_GROUPS * D_STATE

ATT_Q_HEADS = 16
ATT_KV_HEADS = 4
ATT_REP = ATT_Q_HEADS // ATT_KV_HEADS
ATT_HEAD_DIM = 64
D_ATT = ATT_Q_HEADS * ATT_HEAD_DIM
D_KV = ATT_KV_HEADS * ATT_HEAD_DIM
WINDOW = 128
WINDOW_CHUNKS = WINDOW // CHUNK
BAND = (WINDOW_CHUNKS + 1) * CHUNK
ROPE_THETA = 10000.0

D_MIX = D_SSD + D_ATT
D_IN_PROJ = D_SSD + D_CONV + SSD_HEADS + D_ATT + 2 * D_KV + D_ATT

kernel_name = "hymba_ssd_swa_sink_streaming_layer"


def rmsnorm(x, w):
    x32 = x.astype(jnp.float32)
    y = x32 * lax.rsqrt(jnp.mean(x32 * x32, axis=-1, keepdims=True) + EPS)
    return (y * w.astype(jnp.float32)).astype(x.dtype)


def causal_depthwise_conv(u, w, b):
    out = lax.conv_general_dilated(
        u.astype(jnp.float32), w.astype(jnp.float32)[:, None, :],
        window_strides=(1,), padding=[(CONV_WIDTH - 1, 0)],
        dimension_numbers=("NWC", "WIO", "NWC"), feature_group_count=u.shape[-1])
    return out + b.astype(jnp.float32)


def ssd_chunked_scan(xs, dt, a, b_mat, c_mat):
    bsz, lp, g, r, p = xs.shape
    n = b_mat.shape[-1]
    nc = lp // CHUNK
    xs = xs.reshape(bsz, nc, CHUNK, g, r, p)
    dt = dt.reshape(bsz, nc, CHUNK, g, r)
    b_mat = b_mat.reshape(bsz, nc, CHUNK, g, n)
    c_mat = c_mat.reshape(bsz, nc, CHUNK, g, n)
    a_cs = jnp.cumsum(dt * a, axis=2)
    x_dt = xs * dt[..., None]
    causal = jnp.tril(jnp.ones((CHUNK, CHUNK), dtype=bool))[:, :, None, None]
    seg = a_cs[:, :, :, None] - a_cs[:, :, None, :]
    decay_ls = jnp.exp(jnp.where(causal, seg, -jnp.inf))
    cb = jnp.einsum("bclgn,bcsgn->bclsg", c_mat, b_mat)
    y_diag = jnp.einsum("bclsg,bclsgr,bcsgrp->bclgrp", cb, decay_ls, x_dt)
    decay_to_end = jnp.exp(a_cs[:, :, -1:] - a_cs)
    states = jnp.einsum("bclgn,bclgr,bclgrp->bcgrpn", b_mat, decay_to_end, x_dt)
    chunk_decay = jnp.exp(a_cs[:, :, -1])

    def step(h, inp):
        st, dec = inp
        return h * dec[..., None, None] + st, h

    h0 = jnp.zeros((bsz, g, r, p, n), xs.dtype)
    _, prev = lax.scan(step, h0, (jnp.moveaxis(states, 1, 0), jnp.moveaxis(chunk_decay, 1, 0)))
    prev = jnp.moveaxis(prev, 0, 1)
    y_off = jnp.einsum("bclgn,bcgrpn,bclgr->bclgrp", c_mat, prev, jnp.exp(a_cs))
    return (y_diag + y_off).reshape(bsz, lp, g, r, p)


def rope(t, pos):
    half = t.shape[-1] // 2
    inv = ROPE_THETA ** (-jnp.arange(half, dtype=jnp.float32) / half)
    ang = pos.astype(jnp.float32)[:, None] * inv[None, :]
    cos = jnp.cos(ang)[None, :, None, :]
    sin = jnp.sin(ang)[None, :, None, :]
    t1, t2 = t[..., :half], t[..., half:]
    return jnp.concatenate([t1 * cos - t2 * sin, t1 * sin + t2 * cos], axis=-1)


def banded_sink_attention(q, k, v, sinks):
    bsz, lp = q.shape[:2]
    nc = lp // CHUNK
    q = q.reshape(bsz, nc, CHUNK, ATT_KV_HEADS, ATT_REP, ATT_HEAD_DIM)
    k = k.reshape(bsz, nc, CHUNK, ATT_KV_HEADS, ATT_HEAD_DIM)
    v = v.reshape(bsz, nc, CHUNK, ATT_KV_HEADS, ATT_HEAD_DIM)
    padw = ((0, 0), (WINDOW_CHUNKS, 0), (0, 0), (0, 0), (0, 0))
    kp, vp = jnp.pad(k, padw), jnp.pad(v, padw)
    k_band = jnp.concatenate([kp[:, j:j + nc] for j in range(WINDOW_CHUNKS + 1)], axis=2)
    v_band = jnp.concatenate([vp[:, j:j + nc] for j in range(WINDOW_CHUNKS + 1)], axis=2)
    scale = ATT_HEAD_DIM ** -0.5
    s = jnp.einsum("bclhrd,bcshd->bchrls", q, k_band) * scale
    key_abs = (jnp.arange(nc)[:, None] - WINDOW_CHUNKS) * CHUNK + jnp.arange(BAND)[None, :]
    valid = key_abs >= PAD_LEAD
    s = jnp.where(valid[None, :, None, None, None, :], s, -jnp.inf)
    sink = sinks.astype(jnp.float32).reshape(ATT_KV_HEADS, ATT_REP)[None, None, :, :, None, None]
    m = jnp.maximum(jnp.max(s, axis=-1, keepdims=True), sink)
    pr = jnp.exp(s - m)
    denom = jnp.sum(pr, axis=-1, keepdims=True) + jnp.exp(sink - m)
    out = jnp.einsum("bchrls,bcshd->bclhrd", pr / denom, v_band)
    return out.reshape(bsz, lp, D_ATT)


def setup_inputs(seed: int = 0) -> dict:
    key = jax.random.key(seed)
    ks = jax.random.split(key, 14)
    f32 = jnp.float32
    x = jax.random.normal(ks[0], (BATCH, SEQ, D_MODEL), f32)
    meta_tokens = jax.random.normal(ks[1], (N_META, D_MODEL), f32)
    norm_pre_w = 1.0 + 0.01 * jax.random.normal(ks[2], (DEPTH, D_MODEL), f32)
    w_in = jax.random.normal(ks[3], (DEPTH, D_MODEL, D_IN_PROJ), f32) * D_MODEL ** -0.5
    conv_w = jax.random.normal(ks[4], (DEPTH, CONV_WIDTH, D_CONV), f32) * CONV_WIDTH ** -0.5
    conv_b = 0.02 * jax.random.normal(ks[5], (DEPTH, D_CONV), f32)
    dt0 = jnp.exp(jax.random.uniform(ks[6], (DEPTH, SSD_HEADS), f32,
                                     minval=math.log(1e-3), maxval=math.log(1e-1)))
    dt_bias = dt0 + jnp.log(-jnp.expm1(-dt0))
    a_log = jnp.log(jax.random.uniform(ks[7], (DEPTH, SSD_HEADS), f32, minval=1.0, maxval=16.0))
    d_skip = 1.0 + 0.01 * jax.random.normal(ks[8], (DEPTH, SSD_HEADS), f32)
    ssd_norm_w = 1.0 + 0.01 * jax.random.normal(ks[9], (DEPTH, D_SSD), f32)
    attn_sinks = 0.5 * jax.random.normal(ks[10], (DEPTH, ATT_Q_HEADS), f32)
    w_out = jax.random.normal(ks[11], (DEPTH, D_MIX, D_MODEL), f32) * D_MIX ** -0.5
    norm_post_w = 1.0 + 0.01 * jax.random.normal(ks[12], (DEPTH, D_MODEL), f32)
    return {"x": x, "meta_tokens": meta_tokens, "norm_pre_w": norm_pre_w, "w_in": w_in,
            "conv_w": conv_w, "conv_b": conv_b, "dt_bias": dt_bias, "a_log": a_log,
            "d_skip": d_skip, "ssd_norm_w": ssd_norm_w, "attn_sinks": attn_sinks,
            "w_out": w_out, "norm_post_w": norm_post_w}


def reference(x, meta_tokens, norm_pre_w, w_in, conv_w, conv_b, dt_bias, a_log, d_skip,
              ssd_norm_w, attn_sinks, w_out, norm_post_w):
    bsz, seq, _ = x.shape
    meta = jnp.broadcast_to(meta_tokens[None].astype(x.dtype), (bsz, N_META, D_MODEL))
    h = jnp.concatenate([meta, x], axis=1)
    lp = N_META + seq + PAD_LEAD
    idx = jnp.arange(lp)
    pos = idx - PAD_LEAD
    valid = (idx >= PAD_LEAD).astype(jnp.float32)
    split_pts = [D_SSD, D_SSD + D_CONV, D_SSD + D_CONV + SSD_HEADS,
                 D_SSD + D_CONV + SSD_HEADS + D_ATT,
                 D_SSD + D_CONV + SSD_HEADS + D_ATT + D_KV,
                 D_SSD + D_CONV + SSD_HEADS + D_ATT + 2 * D_KV]
    for layer in range(DEPTH):
        hn = rmsnorm(h, norm_pre_w[layer])
        proj = jnp.matmul(hn, w_in[layer]).astype(jnp.float32)
        proj = jnp.pad(proj, ((0, 0), (PAD_LEAD, 0), (0, 0)))
        z, xbc, dt_raw, q, k, v, g_att = jnp.split(proj, split_pts, axis=-1)

        xbc = jax.nn.silu(causal_depthwise_conv(xbc, conv_w[layer], conv_b[layer]))
        xs, b_mat, c_mat = jnp.split(xbc, [D_SSD, D_SSD + SSD_GROUPS * D_STATE], axis=-1)
        dt = jax.nn.softplus(dt_raw + dt_bias[layer].astype(jnp.float32)) * valid[None, :, None]
        a = -jnp.exp(a_log[layer].astype(jnp.float32))
        xs = xs.reshape(bsz, lp, SSD_GROUPS, SSD_HEADS_PER_GROUP, SSD_HEAD_DIM)
        y = ssd_chunked_scan(xs, dt.reshape(bsz, lp, SSD_GROUPS, SSD_HEADS_PER_GROUP),
                             a.reshape(SSD_GROUPS, SSD_HEADS_PER_GROUP),
                             b_mat.reshape(bsz, lp, SSD_GROUPS, D_STATE),
                             c_mat.reshape(bsz, lp, SSD_GROUPS, D_STATE))
        y = y + d_skip[layer].astype(jnp.float32).reshape(SSD_GROUPS, SSD_HEADS_PER_GROUP)[..., None] * xs
        y = y.reshape(bsz, lp, D_SSD) * jax.nn.silu(z)
        y = rmsnorm(y.reshape(bsz, lp, SSD_GROUPS, D_SSD // SSD_GROUPS),
                    jnp.ones((D_SSD // SSD_GROUPS,), jnp.float32)).reshape(bsz, lp, D_SSD)
        y = y * ssd_norm_w[layer].astype(jnp.float32)

        q = rope(q.reshape(bsz, lp, ATT_Q_HEADS, ATT_HEAD_DIM), pos)
        k = rope(k.reshape(bsz, lp, ATT_KV_HEADS, ATT_HEAD_DIM), pos)
        v = v.reshape(bsz, lp, ATT_KV_HEADS, ATT_HEAD_DIM)
        att = banded_sink_attention(q, k, v, attn_sinks[layer]) * jax.nn.silu(g_att)

        mix = jnp.concatenate([y, att], axis=-1)[:, PAD_LEAD:].astype(h.dtype)
        out = jnp.matmul(mix, w_out[layer])
        h = h + rmsnorm(out, norm_post_w[layer])
    return h[:, N_META:]
```

```python
import os
import numpy as np
from contextlib import ExitStack
import concourse.bass as bass
import concourse.mybir as mybir
from concourse.bass_utils import run_bass_kernel_spmd

F32 = mybir.dt.float32
BF16 = mybir.dt.bfloat16
ALU = mybir.AluOpType
AF = mybir.ActivationFunctionType

D = 2048
DIN = 8736
DMIX = 3072
SEQ = 4096
NMETA = 16
PADL = 48
OFF_Z, OFF_XS, OFF_B, OFF_C, OFF_DT, OFF_Q, OFF_K, OFF_V, OFF_G = 0, 2048, 4096, 5120, 6144, 6176, 7200, 7456, 7712
EPS = 1e-6
NEG = -30000.0


class Buf:
    __slots__ = ("name", "last_w", "readers", "dma_cnt")

    def __init__(self, name):
        self.name = name
        self.last_w = None
        self.readers = {}
        self.dma_cnt = 0


class Sched:
    ENGS = ("pe", "act", "dve", "pool", "sp")

    def __init__(self, same_engine_sync=True):
        self.ops = {e: [] for e in self.ENGS}
        self.seq = {e: 0 for e in self.ENGS}
        self.waited = {e: {} for e in self.ENGS}
        self.semkeys = []
        self.cur = {}
        self.same = same_engine_sync

    def _need(self, eng, key, val):
        if key == eng and (eng == "pe" or (self.same is not True and eng not in self.same)):
            return
        if self.waited[eng].get(key, 0) >= val:
            return
        self.waited[eng][key] = val
        self.ops[eng].append(("wait", key, val))

    def _deps(self, eng, reads, writes):
        for b in reads:
            if b.last_w is not None:
                self._need(eng, *b.last_w)
        for b in writes:
            if b.last_w is not None:
                self._need(eng, *b.last_w)
            for k, v in b.readers.items():
                self._need(eng, k, v)

    def _record(self, tok, reads, writes):
        for b in reads:
            if b.readers.get(tok[0], 0) < tok[1]:
                b.readers[tok[0]] = tok[1]
        for b in writes:
            b.last_w = tok
            b.readers = {}

    def op(self, eng, fn, reads=(), writes=(), inc=True, sreads=()):
        if sreads:
            sv = self.same
            self.same = True
            for b in sreads:
                if b.last_w is not None:
                    self._need(eng, *b.last_w)
            self.same = sv
        self._deps(eng, reads, writes)
        if inc:
            self.seq[eng] += 1
            tok = (eng, self.seq[eng])
        else:
            tok = (eng, self.seq[eng] + 1)
        if eng not in self.semkeys:
            self.semkeys.append(eng)
        self.cur[eng] = self.seq[eng]
        self._record(tok, reads, writes)
        self.ops[eng].append(("op", fn, eng if inc else None, 1))

    def dma(self, eng, fn, sb, reads=(), writes=()):
        self._deps(eng, reads, writes)
        key = "d_" + sb.name
        if key not in self.semkeys:
            self.semkeys.append(key)
        sb.dma_cnt += 1
        tok = (key, 16 * sb.dma_cnt)
        self.cur[key] = tok[1]
        self._record(tok, reads, writes)
        self.ops[eng].append(("op", fn, key, 16))

    def barrier(self):
        for e in self.ENGS:
            for k, v in self.cur.items():
                if v > 0:
                    self._need(e, k, v)

    def final_wait(self, eng, bufs):
        for b in bufs:
            if b.last_w is not None:
                self._need(eng, *b.last_w)
            for k, v in b.readers.items():
                self._need(eng, k, v)

    def emit(self, nc):
        with ExitStack() as es:
            sems = {}
            for i, k in enumerate(self.semkeys):
                sems[k] = es.enter_context(nc.semaphore("s%d_%s" % (i, k)))
            block = es.enter_context(nc.Block())

            def run(engname):
                def body(eng):
                    for o in self.ops[engname]:
                        if o[0] == "wait":
                            eng.wait_ge(sems[o[1]], o[2])
                        else:
                            ins = o[1](eng)
                            if o[2] is not None:
                                ins.then_inc(sems[o[2]], o[3])
                return body

            block.tensor(run("pe"))
            block.scalar(run("act"))
            block.vector(run("dve"))
            block.gpsimd(run("pool"))
            block.sync(run("sp"))


def split_blocks(n, mx):
    nb = (n + mx - 1) // mx
    base, rem = divmod(n, nb)
    return [base + (1 if i < rem else 0) for i in range(nb)]


def build_program(O, pre_max=8, own_max=6, same_sync=True):
    P = O - 1
    NT = P + O
    pre_blocks = split_blocks(P, pre_max) if P > 0 else []
    own_blocks = split_blocks(O, own_max)
    NTBA = max(pre_blocks + own_blocks)
    NTBO = max(own_blocks)

    nc = bass.Bass("TRN2", target_bir_lowering=False)
    xin = nc.dram_tensor("xin", [NT * 128, D], F32, kind="ExternalInput").ap()
    tokc = nc.dram_tensor("tokc", [NT * 128, 66], F32, kind="ExternalInput").ap()
    kball = nc.dram_tensor("kball", [128, NT], F32, kind="ExternalInput").ap()
    w_in = nc.dram_tensor("w_in", [D, DIN], F32, kind="ExternalInput").ap()
    w_out = nc.dram_tensor("w_out", [DMIX, D], F32, kind="ExternalInput").ap()
    cst_d = nc.dram_tensor("cst", [128, 768], F32, kind="ExternalInput").ap()
    pfm_d = nc.dram_tensor("pfm", [128, 192], F32, kind="ExternalInput").ap()
    prow_d = nc.dram_tensor("prow", [1, 2160], F32, kind="ExternalInput").ap()
    out_d = nc.dram_tensor("out", [O * 128, D], F32, kind="ExternalOutput").ap()

    w_in_v = w_in.rearrange("(kc p) c -> p kc c", p=128)
    w_out_v = w_out.rearrange("(m p) d -> p m d", p=128)

    S = Sched(same_engine_sync=same_sync)
    es = ExitStack()
    TOTAL = 212000
    arena = es.enter_context(nc.sbuf_tensor("arena", [128, TOTAL // 2], BF16))
    state_off = [0]

    def alloc(shape, dt):
        n = int(np.prod(shape))
        nbytes = n * (4 if dt == F32 else 2)
        o = (state_off[0] + 63) // 64 * 64
        state_off[0] = o + nbytes
        assert state_off[0] <= TOTAL, ("SBUF overflow", state_off[0])
        v = arena[:, o // 2:(o + nbytes) // 2]
        if dt == F32:
            v = v.bitcast(F32)
        if len(shape) == 2:
            v = v.rearrange("p (a b) -> p a b", a=shape[0])
        elif len(shape) == 3:
            v = v.rearrange("p (a b c) -> p a b c", a=shape[0], b=shape[1])
        return v

    def ps(name, shape, dt):
        return es.enter_context(nc.psum_tensor(name, shape, dt))

    def tt(out, in0, in1, op, r, w, eng="dve"):
        S.op(eng, lambda e: e.tensor_tensor(out=out, in0=in0, in1=in1, op=op), r, w)

    def ts(out, in0, s1, s2, op0, op1, r, w, eng="dve"):
        sr = r if not isinstance(s1, (int, float)) or not isinstance(s2, (int, float, type(None))) else ()
        if s2 is None:
            S.op(eng, lambda e: e.tensor_scalar(out=out, in0=in0, scalar1=s1, scalar2=None, op0=op0), r, w, sreads=sr)
        else:
            S.op(eng, lambda e: e.tensor_scalar(out=out, in0=in0, scalar1=s1, scalar2=s2, op0=op0, op1=op1), r, w, sreads=sr)

    def stt(out, in0, scalar, in1, op0, op1, r, w, eng="dve"):
        sr = r if not isinstance(scalar, (int, float)) else ()
        S.op(eng, lambda e: e.scalar_tensor_tensor(out=out, in0=in0, scalar=scalar, in1=in1, op0=op0, op1=op1), r, w, sreads=sr)

    def cp(out, in_, r, w, eng="dve"):
        S.op(eng, lambda e: e.tensor_copy(out=out, in_=in_), r, w)

    def recip(out, in_, r, w):
        S.op("dve", lambda e: e.reciprocal(out=out, in_=in_), r, w)

    def act(out, in_, func, r, w, bias=None, scale=None, accum=None):
        kw = {}
        if bias is not None:
            kw["bias"] = bias
        if scale is not None:
            kw["scale"] = scale
        if accum is not None:
            kw["accum_out"] = accum
        S.op("act", lambda e: e.activation(out=out, in_=in_, func=func, **kw), r, w)

    import os
    USE_SILU = os.environ.get("K_SILU", "0") == "1"

    def silu_mul(out, x, tmp, rx, btmp, wout_):
        if USE_SILU:
            act(out, x, AF.Silu, rx, wout_)
            return
        act(tmp, x, AF.Exp, rx, [btmp], scale=-1.0)
        act(tmp, tmp, AF.Ln, [btmp], [btmp], bias=1.0)
        act(tmp, tmp, AF.Exp, [btmp], [btmp], scale=-1.0)
        tt(out, x, tmp, ALU.mult, rx + [btmp], wout_)

    def rstd_of(dst, tmp, ssq, n, bst):
        act(tmp, ssq, AF.Ln, [bst, b_epsc], [bst], scale=1.0 / n, bias=epsc[:, 0:1])
        act(dst, tmp, AF.Exp, [bst], [bst], scale=-0.5)

    def mm(out, lhsT, rhs, start, stop, r, w, inc=True):
        S.op("pe", lambda e: e.matmul(out, lhsT=lhsT, rhs=rhs, start=start, stop=stop), r, w, inc=inc)

    def tp(out, in_, ident, r, w, inc=True):
        S.op("pe", lambda e: e.transpose(out=out, in_=in_, identity=ident), r, w, inc=inc)

    def dma(q, out, in_, sb, r, w):
        S.dma(q, lambda e: e.dma_start(out=out, in_=in_), sb, r, w)

    def bc(ap2, n):
        return ap2.unsqueeze(2).to_broadcast([128, ap2.shape[1], n])

    def bc1(ap2, k):
        return ap2.unsqueeze(1).to_broadcast([128, k, ap2.shape[1]])

    ipA = ps("ipA", [128, 512], F32); b_ipA = Buf("ipA")
    ipB = ps("ipB", [128, 512], F32); b_ipB = Buf("ipB")
    pT = ps("pT", [128, 1024], BF16); b_pT = Buf("pT")
    CBs = ps("CBs", [128, 512], F32); b_CBs = Buf("CBs")
    YY = ps("YY", [128, 512], F32); b_YY = Buf("YY")
    SN = ps("SN", [128, 512], F32); b_SN = Buf("SN")
    ST = ps("ST", [128, 512], F32); b_ST = Buf("ST")
    PV = ps("PV", [128, 512], F32); b_PV = Buf("PV")
    ipbanks = [(ipA, b_ipA), (ipB, b_ipB), (SN, b_SN), (CBs, b_CBs), (YY, b_YY), (ST, b_ST), (PV, b_PV)]
    ipc = [0]

    ippool = [3]

    def next_ip():
        r = ipbanks[ipc[0] % ippool[0]]
        ipc[0] += 1
        return r

    cst = alloc([768], F32); b_cst = Buf("cst")
    ident_f = cst[:, 0:128]; Ubd = cst[:, 128:256]; Tincl = cst[:, 256:384]
    T2 = cst[:, 384:448]; tri = cst[:, 448:512]; csel = [cst[:, 512:640], cst[:, 640:768]]
    identb = alloc([128], BF16); b_identb = Buf("identb")
    pfm = alloc([192], F32); b_pfm = Buf("pfm")
    nwpre = pfm[:, 0:16]; ssdnw = pfm[:, 16:32]
    convw = pfm[:, 32:160].rearrange("p (a b) -> p a b", a=32); convb = pfm[:, 160:192]
    prow = alloc([2160], F32); b_prow = Buf("prow")
    dtb_r = prow[:, 0:32]; a_r = prow[:, 32:64]; dskip_r = prow[:, 64:96]; esink_r = prow[:, 96:112]
    nwpost = prow[:, 112:2160]
    kb = alloc([NT], F32); b_kb = Buf("kb")
    carry = alloc([32, 3], F32); b_carry = [Buf("carry%d" % i) for i in range(32)]
    state = alloc([8, 256], F32); b_state = [Buf("state%d" % i) for i in range(8)]
    kT = [[alloc([128], BF16) for _ in range(2)] for _ in range(4)]
    b_kT = [[Buf("kT%d_%d" % (k, s)) for s in range(2)] for k in range(4)]
    vaug = [[alloc([65], BF16) for _ in range(2)] for _ in range(4)]
    b_vaug = [[Buf("va%d_%d" % (k, s)) for s in range(2)] for k in range(4)]
    mixT_off = (state_off[0] + 63) // 64 * 64
    mixT = alloc([24, NTBO * 128], BF16); b_mixT = [Buf("mixT%d" % i) for i in range(NTBO)]
    _save = state_off[0]
    state_off[0] = mixT_off
    NTBP = max(pre_blocks) if pre_blocks else 1
    xsB_all = alloc([NTBP, 384], BF16); b_xsBall = Buf("xsBall")
    xdW_all = alloc([NTBP, 256], BF16); b_xdW = Buf("xdW")
    lsum = alloc([2 * NTBP, 32], F32)
    Wlog = alloc([2 * NTBP, 32], F32)
    Wfac = alloc([2 * NTBP, 32], F32)
    Ffac = alloc([NTBP, 32], F32)
    Dtot = alloc([32], F32)
    b_pfx = Buf("pfx")
    assert state_off[0] <= _save
    state_off[0] = _save
    tokb = alloc([NTBA, 66], F32); b_tokb = Buf("tokb")
    NX = 1
    xbuf = [alloc([D], F32) for _ in range(NX)]; b_xbuf = [Buf("xbuf%d" % i) for i in range(NX)]
    stt_ = [alloc([8], F32) for _ in range(NX)]; b_stt = [Buf("st%d" % i) for i in range(NX)]
    epsc = alloc([8], F32); b_epsc = Buf("epsc")
    negb = alloc([32], F32); b_negb = Buf("negb")
    mark = state_off[0]

    hnT = alloc([16, NTBA * 128], BF16); b_hnT = [Buf("hnT%d" % i) for i in range(NTBA)]
    hnb = alloc([D], BF16); b_hnb = Buf("hnb")
    NSLOT = 4
    ring = [alloc([4096], BF16) for _ in range(NSLOT)]; b_ring = [Buf("ring%d" % i) for i in range(NSLOT)]
    aq = alloc([16, 256], BF16); b_aq = Buf("aq")
    akv = alloc([16, 128], BF16); b_akv = Buf("akv")
    ag = alloc([16, 256], BF16); b_ag = Buf("ag")
    wdt = alloc([16, 32], BF16); b_wdt = Buf("wdt")
    ubuf = [alloc([516], BF16) for _ in range(2)]; b_u = [Buf("u0"), Buf("u1")]
    dgb = [alloc([4, 128], BF16) for _ in range(2)]; b_dg = [Buf("dg0"), Buf("dg1")]
    silt = [alloc([512], F32) for _ in range(2)]; b_silt = [Buf("silt0"), Buf("silt1")]
    fmT_ = [alloc([4, NTBA * 128], BF16) for _ in range(2)]
    b_fmT_ = [[[Buf("fmT%d_%d_%d" % (q, c, i)) for i in range(NTBA)] for c in range(4)] for q in range(2)]
    dtall = alloc([NTBA, 224], F32); b_dt = [Buf("dt%d" % i) for i in range(NTBA)]
    dtmp = alloc([6, 32], F32); b_dtmp = Buf("dtmp")
    def dup(shape, dt, name):
        return [alloc(shape, dt) for _ in range(2)], [Buf(name + "0"), Buf(name + "1")]
    xsB_, b_xsB_ = dup([384], BF16, "xsB")
    sz_, b_sz_ = dup([256], F32, "sz")
    mCB_, b_mCB_ = dup([64], F32, "mCB")
    Rt_, b_R_ = dup([256], F32, "R")
    Et_, b_E_ = dup([256], F32, "E")
    MT_, b_MT_ = dup([4, 64], BF16, "MT")
    xdt_, b_xdt_ = dup([256], BF16, "xdt")
    xdte_, b_xdte_ = dup([256], BF16, "xdte")
    Sb_, b_Sb_ = dup([256], BF16, "Sb")
    t1_, b_t1_ = dup([256], F32, "t1")
    t2_, b_t2_ = dup([256], F32, "t2")
    yfin_, b_yfin_ = dup([256], BF16, "yfin")
    gn_, b_gn_ = dup([8], F32, "gn")
    YYs = [(YY, b_YY), (CBs, b_CBs)]
    qkf = alloc([320], F32); b_qkf = Buf("qkf")
    sg = alloc([256], F32); b_sg = Buf("sg")
    ra = alloc([160], F32); b_ra = Buf("ra")
    rb = alloc([160], F32); b_rb = Buf("rb")
    qkb = alloc([320], BF16); b_qkb = Buf("qkb")
    qT = alloc([4, 128], BF16); b_qT = Buf("qT")
    PT = alloc([512], BF16); b_PT = Buf("PT")
    den = alloc([8], F32); b_den = Buf("den")
    att = alloc([256], F32); b_att = Buf("att")
    attb = alloc([256], BF16); b_attb = Buf("attb")
    endA = state_off[0]
    state_off[0] = mark
    wout = alloc([24, D], BF16); b_wout = [Buf("wout%d" % i) for i in range(4)]
    junk = alloc([512], F32); b_junk = Buf("junk")
    otmp = alloc([512], F32); b_otmp = Buf("otmp")
    endB = state_off[0]
    state_off[0] = max(endA, endB)

    dma("sp", cst, cst_d, b_cst, [], [b_cst])
    dma("sp", pfm, pfm_d, b_pfm, [], [b_pfm])
    dma("sp", prow, prow_d.partition_broadcast(128), b_prow, [], [b_prow])
    dma("sp", kb, kball, b_kb, [], [b_kb])
    cp(identb, ident_f, [b_cst], [b_identb])
    act(a_r, a_r, AF.Exp, [b_prow], [b_prow])
    ts(a_r, a_r, -1.0, None, ALU.mult, None, [b_prow], [b_prow])
    act(esink_r, esink_r, AF.Exp, [b_prow], [b_prow])
    S.op("dve", lambda e: e.memset(epsc, EPS), [], [b_epsc])
    ts(negb, convb, -1.0, None, ALU.mult, None, [b_pfm], [b_negb])
    S.op("dve", lambda e: e.memset(carry, 0.0), [], b_carry)
    S.op("dve", lambda e: e.memset(state, 0.0), [], b_state)
    for k in range(4):
        for s in range(2):
            S.op("dve", lambda e, k=k, s=s: e.memset(vaug[k][s], 1.0), [], [b_vaug[k][s]])
            S.op("dve", lambda e, k=k, s=s: e.memset(kT[k][s], 0.0), [], [b_kT[k][s]])

    ringc = [0]

    def load_w(pieces):
        s = ringc[0] % NSLOT
        ringc[0] += 1
        ntot = sum(p[1] for p in pieces)
        view = ring[s][:, 0:16 * ntot].rearrange("p (k c) -> p k c", k=16)
        o = 0
        for (c0, n) in pieces:
            dma("pool", view[:, :, o:o + n], w_in_v[:, :, c0:c0 + n], b_ring[s], [], [b_ring[s]])
            o += n
        return view, b_ring[s]

    xc = [0]

    def stage0(i, t):
        s = xc[0] % NX
        xc[0] += 1
        xb_, bx = xbuf[s], b_xbuf[s]
        st, bst = stt_[s], b_stt[s]
        dma("sp", xb_, xin[t * 128:(t + 1) * 128, :], bx, [], [bx])
        act(hnb, xb_, AF.Square, [bx], [b_hnb, bst], accum=st[:, 0:1])
        rstd_of(st[:, 3:4], st[:, 2:3], st[:, 0:1], D, bst)
        ts(hnb, xb_, st[:, 3:4], None, ALU.mult, None, [bx, bst], [b_hnb])
        for h in range(2):
            for j in range(8):
                kc = h * 8 + j
                tp(pT[:, j * 128:(j + 1) * 128], hnb[:, kc * 128:(kc + 1) * 128], identb,
                   [b_hnb, b_identb], [b_pT], inc=(j == 7))
            tt(hnT[:, h * 8:(h + 1) * 8, i * 128:(i + 1) * 128],
               pT[:, 0:1024].rearrange("p (a b) -> p a b", a=8),
               bc(nwpre[:, h * 8:(h + 1) * 8], 128), ALU.mult, [b_pT, b_pfm], [b_hnT[i]])

    def stage1(i, pre=False):
        ip, bip = next_ip()
        for kc in range(16):
            mm(ip[:, 0:32], hnT[:, kc, i * 128:(i + 1) * 128], wdt[:, kc, :], kc == 0, kc == 15,
               [b_hnT[i], b_wdt], [bip], inc=(kc == 15))
        d = dtall[:, i, :]
        bd = b_dt[i]
        xr, ab, ee, ll, mx = dtmp[:, 0, :], dtmp[:, 1, :], dtmp[:, 2, :], dtmp[:, 3, :], dtmp[:, 4, :]
        tt(xr, ip[:, 0:32], dtb_r, ALU.add, [bip, b_prow], [b_dtmp])
        ts(ab, xr, -1.0, None, ALU.mult, None, [b_dtmp], [b_dtmp])
        tt(ab, ab, xr, ALU.min, [b_dtmp], [b_dtmp])
        act(ee, ab, AF.Exp, [b_dtmp], [b_dtmp])
        act(ll, ee, AF.Ln, [b_dtmp], [b_dtmp], bias=1.0)
        ts(mx, xr, 0.0, None, ALU.max, None, [b_dtmp], [b_dtmp])
        tt(mx, mx, ll, ALU.add, [b_dtmp], [b_dtmp])
        ts(d[:, 0:32], mx, tokb[:, i, 0:1], None, ALU.mult, None, [b_dtmp, b_tokb], [bd])
        tt(d[:, 32:64], d[:, 0:32], a_r, ALU.mult, [bd, b_prow], [bd])
        ip2, bip2 = next_ip()
        mm(ip2[:, 0:32], Tincl, d[:, 32:64], True, True, [b_cst, bd], [bip2], inc=False)
        mm(ip2[:, 32:64], Ubd, d[:, 32:64], True, True, [b_cst, bd], [bip2], inc=False)
        mm(ip2[:, 64:96], csel[0], d[:, 32:64], True, True, [b_cst, bd], [bip2], inc=False)
        mm(ip2[:, 96:128], csel[1], d[:, 32:64], True, True, [b_cst, bd], [bip2], inc=True)
        act(d[:, 64:192], ip2[:, 0:128], AF.Exp, [bip2], [bd])
        if pre and os.environ.get('K_NOLS', '0') != '1':
            act(lsum[:, 2 * i:2 * i + 2, :], ip2[:, 64:128].rearrange("p (a b) -> p a b", a=2), AF.Copy, [bip2], [b_pfx])
        tt(d[:, 192:224], d[:, 0:32], d[:, 96:128], ALU.mult, [bd], [bd])

    uc = [0]

    dgc = [0]

    def conv_sub(wview, bw, wc0, ch, fct, t0, n, dslot, gb):
        tl = list(range(t0 // 128, (t0 + n) // 128))
        fmT, b_fmT = fmT_[gb], b_fmT_[gb]
        dg, bdg = dgb[dslot], b_dg[dslot]
        if t0 == 0:
            tt(dg, bc1(identb, 4), bc(convw[:, ch, 0:4], 128), ALU.mult, [b_identb, b_pfm], [bdg])
        ip, bip = next_ip()
        for kc in range(16):
            mm(ip[:, 0:n], wview[:, kc, wc0:wc0 + 128], hnT[:, kc, t0:t0 + n], kc == 0, kc == 15,
               [bw] + [b_hnT[x] for x in tl], [bip], inc=(kc == 15))
        s = uc[0] % 2
        uc[0] += 1
        u, bu = ubuf[s], b_u[s]
        act(u[:, 3:3 + n], ip[:, 0:n], AF.Copy, [bip], [bu])
        cp(u[:, 0:3], carry[:, ch, :], [b_carry[ch]], [bu])
        cp(carry[:, ch, :], u[:, n:n + 3], [bu], [b_carry[ch]])
        yield
        cps, bcps = next_ip()
        for k in range(4):
            mm(cps[:, 0:n], dg[:, k, :], u[:, k:k + n], k == 0, k == 3, [bdg, bu], [bcps], inc=(k == 3))
        tmp, btmp = silt[s][:, 0:n], b_silt[s]
        act(tmp, cps[:, 0:n], AF.Exp, [bcps, b_negb], [btmp], scale=-1.0, bias=negb[:, ch:ch + 1])
        act(tmp, tmp, AF.Ln, [btmp], [btmp], bias=1.0)
        act(tmp, tmp, AF.Exp, [btmp], [btmp], scale=-1.0)
        stt(fmT[:, fct, t0:t0 + n], cps[:, 0:n], convb[:, ch:ch + 1], tmp, ALU.add, ALU.mult, [bcps, b_pfm, btmp],
            [b_fmT[fct][x] for x in tl])
        yield

    def conv_pipe(facts):
        n = len(facts)
        g = [None] * n

        def stepA(k):
            g[k] = facts[k]()
            next(g[k])

        stepA(0)
        yield
        if n > 1:
            stepA(1)
            yield
        for k in range(n):
            next(g[k])
            if k + 2 < n:
                stepA(k + 2)
            yield

    def conv_facts(wview, bw, wc0, ch, fct, tiles_n, gb):
        ntok = tiles_n * 128
        out = []
        t0 = 0
        dslot = dgc[0] % 2
        dgc[0] += 1
        while t0 < ntok:
            n = min(512, ntok - t0)
            out.append(lambda t0=t0, n=n: conv_sub(wview, bw, wc0, ch, fct, t0, n, dslot, gb))
            t0 += n
        return out

    def ssd_tile(i, g, full, wz=None, bwz=None):
        par = i % 2
        fmT, b_fmT = fmT_[g % 2], b_fmT_[g % 2]
        xsB, b_xsB = xsB_[par], b_xsB_[par]
        sz, b_sz = sz_[par], b_sz_[par]
        mCB, b_mCB = mCB_[par], b_mCB_[par]
        Rt, b_R = Rt_[par], b_R_[par]
        Et, b_E = Et_[par], b_E_[par]
        MT, b_MT = MT_[par], b_MT_[par]
        xdt, b_xdt = xdt_[par], b_xdt_[par]
        xdte, b_xdte = xdte_[par], b_xdte_[par]
        Sb, b_Sb = Sb_[par], b_Sb_[par]
        t1, b_t1 = t1_[par], b_t1_[par]
        t2, b_t2 = t2_[par], b_t2_[par]
        yfin, b_yfin = yfin_[par], b_yfin_[par]
        gn, b_gn = gn_[par], b_gn_[par]
        Yp, b_Yp = YYs[par]
        tok = slice(i * 128, (i + 1) * 128)
        d = dtall[:, i, :]
        bd = b_dt[i]
        hs = slice(4 * g, 4 * g + 4)
        if full:
            ip, bip = next_ip()
            for kc in range(16):
                mm(ip[:, 0:256], hnT[:, kc, tok], wz[:, kc, :], kc == 0, kc == 15, [b_hnT[i], bwz], [bip], inc=(kc == 15))
            silu_mul(sz, ip[:, 0:256], sz, [bip], b_sz, [b_sz])
            yield
        for j in range(3):
            tp(pT[:, j * 128:(j + 1) * 128], fmT[:, j, tok], identb, [b_fmT[j][i], b_identb], [b_pT], inc=(j == 2))
        cp(xsB, pT[:, 0:384], [b_pT], [b_xsB])
        yield
        xs4 = xsB[:, 0:256].rearrange("p (a b) -> p a b", a=4)
        tt(xdte.rearrange("p (a b) -> p a b", a=4), xs4, bc(d[:, 192 + 4 * g:196 + 4 * g], 64), ALU.mult, [b_xsB, bd], [b_xdte])
        if full:
            cb, bcb = next_ip()
            mm(cb[:, 0:128], fmT[:, 2, tok], fmT[:, 3, tok], True, True, [b_fmT[2][i], b_fmT[3][i]], [bcb])
            tt(mCB[0:64, :], cb[0:64, 0:64], tri[0:64, :], ALU.mult, [bcb, b_cst], [b_mCB])
            tt(mCB[64:128, :], cb[64:128, 64:128], tri[64:128, :], ALU.mult, [bcb, b_cst], [b_mCB])
            tt(Rt.rearrange("p (a b) -> p a b", a=4), bc1(T2, 4), bc(d[:, 32 + 4 * g:36 + 4 * g], 64), ALU.mult, [b_cst, bd], [b_R])
            yield
            mm(cb[:, 128:384], Ubd, Rt, True, True, [b_cst, b_R], [bcb])
            act(Et, cb[:, 128:384], AF.Exp, [bcb], [b_E])
            tt(MT, Et.rearrange("p (a b) -> p a b", a=4), bc1(mCB, 4), ALU.mult, [b_E, b_mCB], [b_MT])
            tt(xdt.rearrange("p (a b) -> p a b", a=4), xs4, bc(d[:, hs], 64), ALU.mult, [b_xsB, bd], [b_xdt])
            yield
            for c in range(2):
                pr = slice(64 * c, 64 * c + 64)
                for h in range(4):
                    mm(Yp[pr, 64 * h:64 * h + 64], MT[pr, h, :], xdt[pr, 64 * h:64 * h + 64], True, True,
                       [b_MT, b_xdt], [b_Yp], inc=(c == 1 and h == 3))
            yield
        for c in range(2):
            pr = slice(64 * c, 64 * c + 64)
            sn, bsn = next_ip()
            mm(sn[:, 0:256], xsB[pr, 256:384], xdte[pr, :], True, True, [b_xsB, b_xdte], [bsn])
            if full:
                act(Sb, state[:, g, :], AF.Copy, [b_state[g]], [b_Sb])
                mm(Yp[pr, 256:512], fmT[:, 3, i * 128 + 64 * c:i * 128 + 64 * c + 64], Sb, True, True,
                   [b_fmT[3][i], b_Sb], [b_Yp])
            st4 = state[:, g, :].rearrange("p (a b) -> p a b", a=4)
            tt(st4, st4, bc(d[:, 128 + 32 * c + 4 * g:132 + 32 * c + 4 * g], 64), ALU.mult, [b_state[g], bd], [b_state[g]])
            tt(state[:, g, :], state[:, g, :], sn[:, 0:256], ALU.add, [b_state[g], bsn], [b_state[g]])
            yield
        if not full:
            return
        t14 = t1.rearrange("p (a b) -> p a b", a=4)
        tt(t2.rearrange("p (a b) -> p a b", a=4), xs4, bc(dskip_r[:, hs], 64), ALU.mult, [b_xsB, b_prow], [b_t2])
        tt(t14, Yp[:, 256:512].rearrange("p (a b) -> p a b", a=4), bc(d[:, 64 + 4 * g:68 + 4 * g], 64), ALU.mult, [b_Yp, bd], [b_t1])
        tt(t1, t1, Yp[:, 0:256], ALU.add, [b_t1, b_Yp], [b_t1])
        yield
        tt(t1, t1, t2, ALU.add, [b_t1, b_t2], [b_t1])
        tt(t1, t1, sz, ALU.mult, [b_t1, b_sz], [b_t1])
        act(t2, t1, AF.Square, [b_t1], [b_t2, b_gn], accum=gn[:, 0:1])
        yield
        rstd_of(gn[:, 3:4], gn[:, 2:3], gn[:, 0:1], 256, b_gn)
        ts(yfin, t1, gn[:, 3:4], None, ALU.mult, None, [b_t1, b_gn], [b_yfin])
        yield
        for j in range(2):
            tp(pT[:, j * 128:(j + 1) * 128], yfin[:, j * 128:(j + 1) * 128], identb, [b_yfin, b_identb], [b_pT], inc=(j == 1))
        tt(mixT[:, 2 * g:2 * g + 2, tok], pT[:, 0:256].rearrange("p (a b) -> p a b", a=2),
           bc(ssdnw[:, 2 * g:2 * g + 2], 128), ALU.mult, [b_pT, b_pfm], [b_mixT[i]])
        yield

    def prefix_factors(n):
        nc2 = 2 * n
        S.op("dve", lambda e: e.memset(Wlog[:, nc2 - 1, :], 0.0), [], [b_pfx])
        for c in range(nc2 - 2, -1, -1):
            tt(Wlog[:, c, :], Wlog[:, c + 1, :], lsum[:, c + 1, :], ALU.add, [b_pfx], [b_pfx])
        tt(Dtot, Wlog[:, 0, :], lsum[:, 0, :], ALU.add, [b_pfx], [b_pfx])
        act(Wfac[:, 0:nc2, :], Wlog[:, 0:nc2, :], AF.Exp, [b_pfx], [b_pfx])
        act(Dtot, Dtot, AF.Exp, [b_pfx], [b_pfx])
        for i in range(n):
            for cc in range(2):
                pr = slice(64 * cc, 64 * cc + 64)
                tt(Ffac[pr, i, :], dtall[pr, i, 192:224], Wfac[pr, 2 * i + cc, :], ALU.mult, [b_dt[i], b_pfx], [b_pfx])

    def ssd_prefix_group(g, n):
        fmT, b_fmT = fmT_[g % 2], b_fmT_[g % 2]
        for i0 in range(0, n, 2):
            m = min(2, n - i0)
            for ii in range(m):
                for j in range(3):
                    col = (ii * 3 + j) * 128
                    tp(pT[:, col:col + 128], fmT[:, j, (i0 + ii) * 128:(i0 + ii + 1) * 128], identb,
                       [b_fmT[j][i0 + ii], b_identb], [b_pT], inc=(ii == m - 1 and j == 2))
            cp(xsB_all[:, i0:i0 + m, :], pT[:, 0:m * 384].rearrange("p (a b) -> p a b", a=m), [b_pT], [b_xsBall])
            yield
        tt(xdW_all[:, 0:n, :].rearrange("p n (a b) -> p n a b", a=4),
           xsB_all[:, 0:n, 0:256].rearrange("p n (a b) -> p n a b", a=4),
           Ffac[:, 0:n, 4 * g:4 * g + 4].unsqueeze(3).to_broadcast([128, n, 4, 64]), ALU.mult, [b_xsBall, b_pfx], [b_xdW])
        sn, bsn = next_ip()
        for i in range(n):
            mm(sn[:, 0:256], xsB_all[:, i, 256:384], xdW_all[:, i, :], i == 0, i == n - 1, [b_xsBall, b_xdW], [bsn], inc=(i == n - 1))
        st4 = state[:, g, :].rearrange("p (a b) -> p a b", a=4)
        tt(st4, st4, bc(Dtot[:, 4 * g:4 * g + 4], 64), ALU.mult, [b_state[g], b_pfx], [b_state[g]])
        tt(state[:, g, :], state[:, g, :], sn[:, 0:256], ALU.add, [b_state[g], bsn], [b_state[g]])
        yield

    def lagged(factories, lag=2):
        active = []
        idx = 0
        while idx < len(factories) or active:
            if idx < len(factories) and len(active) < 2 and (not active or active[-1][1] >= lag):
                active.append([factories[idx](), 0])
                idx += 1
            for a in list(active):
                try:
                    next(a[0])
                    a[1] += 1
                except StopIteration:
                    active.remove(a)
            yield

    def attn_tile(i, t, kv, kv_only):
        tok = slice(i * 128, (i + 1) * 128)
        cur, prv = t % 2, (t + 1) % 2
        ip, bip = next_ip()
        if not kv_only:
            for kc in range(16):
                mm(ip[:, 0:256], hnT[:, kc, tok], aq[:, kc, :], kc == 0, kc == 15, [b_hnT[i], b_aq], [bip], inc=False)
        for kc in range(16):
            mm(ip[:, 256:384], hnT[:, kc, tok], akv[:, kc, :], kc == 0, kc == 15, [b_hnT[i], b_akv], [bip], inc=(kc == 15))
        if kv_only:
            act(qkf[:, 256:320], ip[:, 256:320], AF.Copy, [bip], [b_qkf])
        else:
            act(qkf, ip[:, 0:320], AF.Copy, [bip], [b_qkf])
        act(vaug[kv][cur][:, 0:64], ip[:, 320:384], AF.Copy, [bip], [b_vaug[kv][cur]])
        yield
        if not kv_only:
            ip2, bip2 = next_ip()
            for kc in range(16):
                mm(ip2[:, 0:256], hnT[:, kc, tok], ag[:, kc, :], kc == 0, kc == 15, [b_hnT[i], b_ag], [bip2], inc=(kc == 15))
            silu_mul(sg, ip2[:, 0:256], sg, [bip2], b_sg, [b_sg])
            yield
        q4 = qkf.rearrange("p (h two f) -> p h two f", h=5, two=2)
        o4 = qkb.rearrange("p (h two f) -> p h two f", h=5, two=2)
        hsl = slice(4, 5) if kv_only else slice(0, 5)
        nh = 1 if kv_only else 5
        q4 = q4[:, hsl]
        o4 = o4[:, hsl]
        q1, q2 = q4[:, :, 0, :], q4[:, :, 1, :]
        cosb = bc1(tokb[:, i, 2:34], nh)
        sinb = bc1(tokb[:, i, 34:66], nh)
        ra3 = ra.rearrange("p (h f) -> p h f", h=5)[:, hsl]
        rb3 = rb.rearrange("p (h f) -> p h f", h=5)[:, hsl]
        tt(ra3, q1, cosb, ALU.mult, [b_qkf, b_tokb], [b_ra])
        tt(rb3, q2, sinb, ALU.mult, [b_qkf, b_tokb], [b_rb])
        tt(o4[:, :, 0, :], ra3, rb3, ALU.subtract, [b_ra, b_rb], [b_qkb])
        tt(ra3, q2, cosb, ALU.mult, [b_qkf, b_tokb], [b_ra])
        tt(rb3, q1, sinb, ALU.mult, [b_qkf, b_tokb], [b_rb])
        tt(o4[:, :, 1, :], ra3, rb3, ALU.add, [b_ra, b_rb], [b_qkb])
        js = [4] if kv_only else [0, 1, 2, 3, 4]
        for j in js:
            tp(pT[0:64, j * 128:(j + 1) * 128], qkb[:, j * 64:(j + 1) * 64], identb, [b_qkb, b_identb], [b_pT], inc=(j == 4))
        act(kT[kv][cur][0:64, :], pT[0:64, 512:640], AF.Copy, [b_pT], [b_kT[kv][cur]])
        if kv_only:
            yield
            return
        act(qT[0:64, :, :], pT[0:64, 0:512].rearrange("p (a b) -> p a b", a=4), AF.Copy, [b_pT], [b_qT])
        yield
        for c in range(2):
            pr = slice(64 * c, 64 * c + 64)
            if c == 0:
                kA, vA, bkA, bvA, tA = kT[kv][prv], vaug[kv][prv], b_kT[kv][prv], b_vaug[kv][prv], t - 1
                kB, vB, bkB, bvB, tB = kT[kv][cur], vaug[kv][cur], b_kT[kv][cur], b_vaug[kv][cur], t
                pB = slice(0, 64)
            else:
                kA, vA, bkA, bvA, tA = kT[kv][cur], vaug[kv][cur], b_kT[kv][cur], b_vaug[kv][cur], t
                kB, vB, bkB, bvB, tB = kT[kv][prv], vaug[kv][prv], b_kT[kv][prv], b_vaug[kv][prv], t - 1
                pB = slice(64, 128)
            qc = qT[0:64, :, 64 * c:64 * c + 64]
            mm(ST[:, 0:256], kA[0:64, :], qc, True, True, [bkA, b_qT], [b_ST], inc=False)
            mm(ST[pB, 256:512], kB[0:64, pB], qc, True, True, [bkB, b_qT], [b_ST])
            act(PT[:, 0:256], ST[:, 0:256], AF.Exp, [b_ST, b_kb], [b_PT], bias=kb[:, tA:tA + 1], scale=0.125)
            act(PT[pB, 256:512], ST[pB, 256:512], AF.Exp, [b_ST, b_kb], [b_PT], bias=kb[pB, tB:tB + 1], scale=0.125)
            yield
            for r in range(4):
                mm(PV[pr, r * 65:(r + 1) * 65], PT[:, r * 64:(r + 1) * 64], vA[:, :], True, False, [b_PT, bvA], [b_PV], inc=False)
                mm(PV[pr, r * 65:(r + 1) * 65], PT[pB, 256 + r * 64:256 + (r + 1) * 64], vB[pB, :], False, True,
                   [b_PT, bvB], [b_PV], inc=(r == 3))
            yield
        pv4 = PV[:, 0:260].rearrange("p (a b) -> p a b", a=4)
        tt(den[:, 0:4], pv4[:, :, 64], esink_r[:, 4 * kv:4 * kv + 4], ALU.add, [b_PV, b_prow], [b_den])
        recip(den[:, 0:4], den[:, 0:4], [b_den], [b_den])
        tt(att.rearrange("p (a b) -> p a b", a=4), pv4[:, :, 0:64], bc(den[:, 0:4], 64), ALU.mult, [b_PV, b_den], [b_att])
        tt(attb, att, sg, ALU.mult, [b_att, b_sg], [b_attb])
        for j in range(2):
            tp(pT[:, j * 128:(j + 1) * 128], attb[:, j * 128:(j + 1) * 128], identb, [b_attb, b_identb], [b_pT], inc=(j == 1))
        act(mixT[:, 16 + 2 * kv:18 + 2 * kv, tok], pT[:, 0:256].rearrange("p (a b) -> p a b", a=2), AF.Copy, [b_pT], [b_mixT[i]])
        yield

    obanks = [(CBs, b_CBs), (YY, b_YY), (SN, b_SN), (ST, b_ST)]

    def outproj_tile(i, t, orow):
        tok = slice(i * 128, (i + 1) * 128)
        s = xc[0] % NX
        xc[0] += 1
        xb_, bx = xbuf[s], b_xbuf[s]
        st, bst = stt_[s], b_stt[s]
        dma("sp", xb_, xin[t * 128:(t + 1) * 128, :], bx, [], [bx])
        for dc in range(4):
            ob, bob = obanks[dc]
            for m in range(24):
                mm(ob[:, 0:512], mixT[:, m, tok], wout[:, m, dc * 512:(dc + 1) * 512], m == 0, m == 23,
                   [b_mixT[i], b_wout[dc]], [bob], inc=(m == 23))
            act(junk, ob[:, 0:512], AF.Square, [bob], [b_junk, bst], accum=st[:, 4 + dc:5 + dc])
        S.op("dve", lambda e: e.reduce_sum(out=st[:, 0:1], in_=st[:, 4:8], axis=mybir.AxisListType.X), [bst], [bst])
        rstd_of(st[:, 3:4], st[:, 2:3], st[:, 0:1], D, bst)
        for dc in range(4):
            ob, bob = obanks[dc]
            cs = slice(dc * 512, (dc + 1) * 512)
            stt(otmp, ob[:, 0:512], st[:, 3:4], nwpost[:, cs], ALU.mult, ALU.mult, [bob, bst, b_prow], [b_otmp])
            tt(xb_[:, cs], xb_[:, cs], otmp, ALU.add, [bx, b_otmp], [bx])
        dma("sp", out_d[orow * 128:(orow + 1) * 128, :], xb_, bx, [bx], [])

    def interleave_gen(chains):
        live = [[g, w, 0] for g, w in chains]
        while live:
            live.sort(key=lambda c: c[2] / c[1])
            c = live[0]
            try:
                next(c[0])
                c[2] += 1
                yield
            except StopIteration:
                live.remove(c)

    def interleave(chains):
        live = [[g, w, 0] for g, w in chains]
        while live:
            live.sort(key=lambda c: c[2] / c[1])
            c = live[0]
            try:
                next(c[0])
                c[2] += 1
            except StopIteration:
                live.remove(c)

    def run_block(tiles, own, last_prefix, own_row0):
        n = len(tiles)
        t0 = tiles[0]
        dma("sp", tokb[:, 0:n, :], tokc[t0 * 128:(t0 + n) * 128, :].rearrange("(n p) c -> p n c", p=128), b_tokb, [], [b_tokb])
        dma("pool", wdt, w_in_v[:, :, OFF_DT:OFF_DT + 32], b_wdt, [], [b_wdt])
        for i, t in enumerate(tiles):
            stage0(i, t)
            if i >= 1:
                stage1(i - 1, not own)
        stage1(n - 1, not own)
        if not own and os.environ.get("K_NOPF", "0") != "1":
            prefix_factors(n)
        needC = own or last_prefix
        ippool[0] = 3 if (own or os.environ.get("K_POOL3", "0") == "1") else 7

        wzs = {}
        nsub_ = (n * 128 + 511) // 512
        w_conv = (4 if needC else 3) * nsub_ + 1
        w_ssd = (5 * n + 5) if own else ((n + 1) // 2 + 1)

        def group_conv(g):
            gb = g % 2
            wx, bwx = load_w([(OFF_XS + g * 256, 256)])
            pcs = [(OFF_B + g * 128, 128)] + ([(OFF_C + g * 128, 128)] if needC else [])
            wbc, bwbc = load_w(pcs)
            if own:
                wzs[g] = load_w([(OFF_Z + g * 256, 256)])
            facts = conv_facts(wx, bwx, 0, 2 * g, 0, n, gb) + conv_facts(wx, bwx, 128, 2 * g + 1, 1, n, gb)
            facts += conv_facts(wbc, bwbc, 0, 16 + g, 2, n, gb)
            if needC:
                facts += conv_facts(wbc, bwbc, 128, 24 + g, 3, n, gb)
            yield from conv_pipe(facts)

        def group_ssd(g):
            if own:
                wz, bwz = wzs[g]
                yield from lagged([(lambda i=i: ssd_tile(i, g, True, wz, bwz)) for i in range(n)])
            else:
                yield from ssd_prefix_group(g, n)

        def chainS():
            yield from group_conv(0)
            for g in range(8):
                subs = [(group_ssd(g), w_ssd)]
                if g < 7:
                    subs.append((group_conv(g + 1), w_conv))
                yield from interleave_gen(subs)

        def load_att(kv, kv_only):
            if not kv_only:
                dma("pool", aq, w_in_v[:, :, OFF_Q + kv * 256:OFF_Q + kv * 256 + 256], b_aq, [], [b_aq])
                dma("pool", ag, w_in_v[:, :, OFF_G + kv * 256:OFF_G + kv * 256 + 256], b_ag, [], [b_ag])
            dma("pool", akv[:, :, 0:64], w_in_v[:, :, OFF_K + kv * 64:OFF_K + kv * 64 + 64], b_akv, [], [b_akv])
            dma("pool", akv[:, :, 64:128], w_in_v[:, :, OFF_V + kv * 64:OFF_V + kv * 64 + 64], b_akv, [], [b_akv])

        def chainA():
            for kv in range(4):
                load_att(kv, False)
                for i, t in enumerate(tiles):
                    yield from attn_tile(i, t, kv, False)

        nsub = (n * 128 + 511) // 512
        if own:
            wS = w_conv + 8 * (w_ssd + w_conv)
            wA = int(4 * n * 9 * 1.35)
            interleave([(chainS(), wS), (chainA(), wA)])
            S.barrier()
            for dc in range(4):
                dma("pool", wout[:, :, dc * 512:(dc + 1) * 512], w_out_v[:, :, dc * 512:(dc + 1) * 512], b_wout[dc], [], [b_wout[dc]])
            for i, t in enumerate(tiles):
                outproj_tile(i, t, own_row0 + i)
            S.barrier()
        else:
            interleave([(chainS(), 1)])
            if last_prefix:
                for kv in range(4):
                    load_att(kv, True)
                    for _ in attn_tile(n - 1, tiles[-1], kv, True):
                        pass

    t = 0
    for bi, nb in enumerate(pre_blocks):
        run_block(list(range(t, t + nb)), False, bi == len(pre_blocks) - 1, 0)
        t += nb
    if os.environ.get("K_NOBAR", "0") != "1":
        S.barrier()
    row0 = 0
    for nb in own_blocks:
        run_block(list(range(t, t + nb)), True, False, row0)
        t += nb
        row0 += nb
    S.final_wait("sp", b_xbuf)
    S.emit(nc)
    es.close()
    return nc


def make_consts():
    k = np.arange(128)
    same = (k[:, None] // 64) == (k[None, :] // 64)
    c = np.zeros((128, 768), np.float32)
    c[:, 0:128] = np.eye(128)
    c[:, 128:256] = (same & (k[:, None] > k[None, :]))
    c[:, 256:384] = (same & (k[:, None] <= k[None, :]))
    l = np.arange(64)
    c[:, 384:448] = ((k[:, None] % 64) <= l[None, :])
    c[:, 448:512] = (l[None, :] >= (k[:, None] % 64))
    c[:, 512:640] = (k[:, None] < 64)
    c[:, 640:768] = (k[:, None] >= 64)
    return c


def kernel(x, meta_tokens, norm_pre_w, w_in, conv_w, conv_b, dt_bias, a_log, d_skip,
           ssd_norm_w, attn_sinks, w_out, norm_post_w, _O=None, _pre_max=6, _own_max=6, _same_sync=("dve", "act")):
    x = np.asarray(x, np.float32)
    bsz, seq, _ = x.shape
    nchunk = (PADL + NMETA + seq) // 64
    G = (nchunk + 1) // 2
    O = (G + 1) // 2 if _O is None else _O
    assert 2 * O - 1 >= G
    P = O - 1
    NT = P + O
    f32 = np.float32
    nslot = (2 * O - 1) * 128
    idx = np.arange(nslot)
    valid = ((idx >= PADL) & (idx < PADL + NMETA + seq)).astype(f32)
    pos = (idx - PADL).astype(f32)
    inv = (10000.0 ** (-np.arange(32, dtype=f32) / f32(32))).astype(f32)
    ang = (pos[:, None] * inv[None, :]).astype(f32)
    tokg = np.zeros((nslot, 66), f32)
    tokg[:, 0] = valid
    tokg[:, 1] = np.where(valid > 0, 0.0, NEG)
    tokg[:, 2:34] = np.cos(ang)
    tokg[:, 34:66] = np.sin(ang)
    meta = np.asarray(meta_tokens, f32)
    cst = make_consts()
    pfm = np.zeros((128, 192), f32)
    pfm[:, 0:16] = np.asarray(norm_pre_w, f32).reshape(16, 128).T
    pfm[:, 16:32] = np.asarray(ssd_norm_w, f32).reshape(16, 128).T
    cw = np.asarray(conv_w, f32).reshape(4, 32, 128)
    pfm[:, 32:160] = cw.transpose(2, 1, 0).reshape(128, 128)
    pfm[:, 160:192] = np.asarray(conv_b, f32).reshape(32, 128).T
    prow = np.concatenate([np.asarray(dt_bias, f32).reshape(-1), np.asarray(a_log, f32).reshape(-1),
                           np.asarray(d_skip, f32).reshape(-1), np.asarray(attn_sinks, f32).reshape(-1),
                           np.asarray(norm_post_w, f32).reshape(-1)]).reshape(1, 2160).astype(f32)
    w_in2 = np.ascontiguousarray(np.asarray(w_in, f32).reshape(D, DIN))
    w_out2 = np.ascontiguousarray(np.asarray(w_out, f32).reshape(DMIX, D))

    in_maps = []
    own_start = []
    for b in range(bsz):
        gx = np.zeros((nslot, D), f32)
        gx[PADL:PADL + NMETA] = meta
        gx[PADL + NMETA:PADL + NMETA + seq] = x[b]
        for j in range(2):
            if j == 0:
                xin = np.concatenate([np.zeros((P * 128, D), f32), gx[:O * 128]], 0)
                tk = np.zeros((NT * 128, 66), f32)
                tk[:P * 128, 1] = NEG
                tk[P * 128:] = tokg[:O * 128]
                own_start.append(0)
            else:
                xin = gx
                tk = tokg
                own_start.append(P * 128)
            kball = np.ascontiguousarray(tk[:, 1].reshape(NT, 128).T)
            in_maps.append({"xin": np.ascontiguousarray(xin), "tokc": np.ascontiguousarray(tk), "kball": kball,
                            "w_in": w_in2, "w_out": w_out2, "cst": cst, "pfm": pfm, "prow": prow})
    ncores = len(in_maps)
    nc = build_program(O, _pre_max, _own_max, _same_sync)
    res = run_bass_kernel_spmd(nc, in_maps, core_ids=list(range(ncores)))
    out = np.zeros((bsz, seq, D), f32)
    for ci in range(ncores):
        b = ci // 2
        r = np.asarray(res.results[ci]["out"])
        s0 = own_start[ci]
        n0 = s0 - (PADL + NMETA)
        lo = max(0, -n0)
        hi = min(O * 128, seq - n0)
        out[b, n0 + lo:n0 + hi] = r[lo:hi]
    return out
```

```python
import os
import numpy as np
from contextlib import ExitStack
import concourse.bass as bass
import concourse.mybir as mybir
from concourse.bass_utils import run_bass_kernel_spmd

F32 = mybir.dt.float32
BF16 = mybir.dt.bfloat16
ALU = mybir.AluOpType
AF = mybir.ActivationFunctionType

D = 2048
DIN = 8736
DMIX = 3072
SEQ = 4096
NMETA = 16
PADL = 48
OFF_Z, OFF_XS, OFF_B, OFF_C, OFF_DT, OFF_Q, OFF_K, OFF_V, OFF_G = 0, 2048, 4096, 5120, 6144, 6176, 7200, 7456, 7712
EPS = 1e-6
NEG = -30000.0


class Buf:
    __slots__ = ("name", "last_w", "readers", "dma_cnt")

    def __init__(self, name):
        self.name = name
        self.last_w = None
        self.readers = {}
        self.dma_cnt = 0


class Sched:
    ENGS = ("pe", "act", "dve", "pool", "sp")

    def __init__(self, same_engine_sync=True):
        self.ops = {e: [] for e in self.ENGS}
        self.seq = {e: 0 for e in self.ENGS}
        self.waited = {e: {} for e in self.ENGS}
        self.semkeys = []
        self.cur = {}
        self.same = same_engine_sync

    def _need(self, eng, key, val):
        if key == eng and (eng == "pe" or (self.same is not True and eng not in self.same)):
            return
        if self.waited[eng].get(key, 0) >= val:
            return
        self.waited[eng][key] = val
        self.ops[eng].append(("wait", key, val))

    def _deps(self, eng, reads, writes):
        for b in reads:
            if b.last_w is not None:
                self._need(eng, *b.last_w)
        for b in writes:
            if b.last_w is not None:
                self._need(eng, *b.last_w)
            for k, v in b.readers.items():
                self._need(eng, k, v)

    def _record(self, tok, reads, writes):
        for b in reads:
            if b.readers.get(tok[0], 0) < tok[1]:
                b.readers[tok[0]] = tok[1]
        for b in writes:
            b.last_w = tok
            b.readers = {}

    def op(self, eng, fn, reads=(), writes=(), inc=True, sreads=()):
        if sreads:
            sv = self.same
            self.same = True
            for b in sreads:
                if b.last_w is not None:
                    self._need(eng, *b.last_w)
            self.same = sv
        self._deps(eng, reads, writes)
        if inc:
            self.seq[eng] += 1
            tok = (eng, self.seq[eng])
        else:
            tok = (eng, self.seq[eng] + 1)
        if eng not in self.semkeys:
            self.semkeys.append(eng)
        self.cur[eng] = self.seq[eng]
        self._record(tok, reads, writes)
        self.ops[eng].append(("op", fn, eng if inc else None, 1))

    def dma(self, eng, fn, sb, reads=(), writes=()):
        self._deps(eng, reads, writes)
        key = "d_" + sb.name
        if key not in self.semkeys:
            self.semkeys.append(key)
        sb.dma_cnt += 1
        tok = (key, 16 * sb.dma_cnt)
        self.cur[key] = tok[1]
        self._record(tok, reads, writes)
        self.ops[eng].append(("op", fn, key, 16))

    def barrier(self):
        for e in self.ENGS:
            for k, v in self.cur.items():
                if v > 0:
                    self._need(e, k, v)

    def final_wait(self, eng, bufs):
        for b in bufs:
            if b.last_w is not None:
                self._need(eng, *b.last_w)
            for k, v in b.readers.items():
                self._need(eng, k, v)

    def emit(self, nc):
        with ExitStack() as es:
            sems = {}
            for i, k in enumerate(self.semkeys):
                sems[k] = es.enter_context(nc.semaphore("s%d_%s" % (i, k)))
            block = es.enter_context(nc.Block())

            def run(engname):
                def body(eng):
                    for o in self.ops[engname]:
                        if o[0] == "wait":
                            eng.wait_ge(sems[o[1]], o[2])
                        else:
                            ins = o[1](eng)
                            if o[2] is not None:
                                ins.then_inc(sems[o[2]], o[3])
                return body

            block.tensor(run("pe"))
            block.scalar(run("act"))
            block.vector(run("dve"))
            block.gpsimd(run("pool"))
            block.sync(run("sp"))


def split_blocks(n, mx):
    nb = (n + mx - 1) // mx
    base, rem = divmod(n, nb)
    return [base + (1 if i < rem else 0) for i in range(nb)]


def build_program(O, pre_max=8, own_max=6, same_sync=True, step_counts=None):
    counts_out = {}
    P = O - 1
    NT = P + O
    pre_blocks = split_blocks(P, pre_max) if P > 0 else []
    own_blocks = split_blocks(O, own_max)
    NTBA = max(pre_blocks + own_blocks)
    NTBO = max(own_blocks)

    nc = bass.Bass("TRN2", target_bir_lowering=False)
    xin = nc.dram_tensor("xin", [NT * 128, D], F32, kind="ExternalInput").ap()
    tokc = nc.dram_tensor("tokc", [NT * 128, 66], F32, kind="ExternalInput").ap()
    kball = nc.dram_tensor("kball", [128, NT], F32, kind="ExternalInput").ap()
    w_in = nc.dram_tensor("w_in", [D, DIN], F32, kind="ExternalInput").ap()
    w_out = nc.dram_tensor("w_out", [DMIX, D], F32, kind="ExternalInput").ap()
    cst_d = nc.dram_tensor("cst", [128, 768], F32, kind="ExternalInput").ap()
    pfm_d = nc.dram_tensor("pfm", [128, 192], F32, kind="ExternalInput").ap()
    prow_d = nc.dram_tensor("prow", [1, 2160], F32, kind="ExternalInput").ap()
    out_d = nc.dram_tensor("out", [O * 128, D], F32, kind="ExternalOutput").ap()

    w_in_v = w_in.rearrange("(kc p) c -> p kc c", p=128)
    w_out_v = w_out.rearrange("(m p) d -> p m d", p=128)

    S = Sched(same_engine_sync=same_sync)
    es = ExitStack()
    TOTAL = 212000
    arena = es.enter_context(nc.sbuf_tensor("arena", [128, TOTAL // 2], BF16))
    state_off = [0]

    def alloc(shape, dt):
        n = int(np.prod(shape))
        nbytes = n * (4 if dt == F32 else 2)
        o = (state_off[0] + 63) // 64 * 64
        state_off[0] = o + nbytes
        assert state_off[0] <= TOTAL, ("SBUF overflow", state_off[0])
        v = arena[:, o // 2:(o + nbytes) // 2]
        if dt == F32:
            v = v.bitcast(F32)
        if len(shape) == 2:
            v = v.rearrange("p (a b) -> p a b", a=shape[0])
        elif len(shape) == 3:
            v = v.rearrange("p (a b c) -> p a b c", a=shape[0], b=shape[1])
        return v

    def ps(name, shape, dt):
        return es.enter_context(nc.psum_tensor(name, shape, dt))

    def tt(out, in0, in1, op, r, w, eng="dve"):
        S.op(eng, lambda e: e.tensor_tensor(out=out, in0=in0, in1=in1, op=op), r, w)

    def ts(out, in0, s1, s2, op0, op1, r, w, eng="dve"):
        sr = r if not isinstance(s1, (int, float)) or not isinstance(s2, (int, float, type(None))) else ()
        if s2 is None:
            S.op(eng, lambda e: e.tensor_scalar(out=out, in0=in0, scalar1=s1, scalar2=None, op0=op0), r, w, sreads=sr)
        else:
            S.op(eng, lambda e: e.tensor_scalar(out=out, in0=in0, scalar1=s1, scalar2=s2, op0=op0, op1=op1), r, w, sreads=sr)

    def stt(out, in0, scalar, in1, op0, op1, r, w, eng="dve"):
        sr = r if not isinstance(scalar, (int, float)) else ()
        S.op(eng, lambda e: e.scalar_tensor_tensor(out=out, in0=in0, scalar=scalar, in1=in1, op0=op0, op1=op1), r, w, sreads=sr)

    def cp(out, in_, r, w, eng="dve"):
        S.op(eng, lambda e: e.tensor_copy(out=out, in_=in_), r, w)

    def recip(out, in_, r, w):
        S.op("dve", lambda e: e.reciprocal(out=out, in_=in_), r, w)

    def act(out, in_, func, r, w, bias=None, scale=None, accum=None):
        kw = {}
        if bias is not None:
            kw["bias"] = bias
        if scale is not None:
            kw["scale"] = scale
        if accum is not None:
            kw["accum_out"] = accum
        S.op("act", lambda e: e.activation(out=out, in_=in_, func=func, **kw), r, w)

    import os
    USE_SILU = os.environ.get("K_SILU", "0") == "1"

    def silu_mul(out, x, tmp, rx, btmp, wout_):
        if USE_SILU:
            act(out, x, AF.Silu, rx, wout_)
            return
        act(tmp, x, AF.Exp, rx, [btmp], scale=-1.0)
        act(tmp, tmp, AF.Ln, [btmp], [btmp], bias=1.0)
        act(tmp, tmp, AF.Exp, [btmp], [btmp], scale=-1.0)
        tt(out, x, tmp, ALU.mult, rx + [btmp], wout_)

    def rstd_of(dst, tmp, ssq, n, bst):
        act(tmp, ssq, AF.Ln, [bst, b_epsc], [bst], scale=1.0 / n, bias=epsc[:, 0:1])
        act(dst, tmp, AF.Exp, [bst], [bst], scale=-0.5)

    def mm(out, lhsT, rhs, start, stop, r, w, inc=True):
        S.op("pe", lambda e: e.matmul(out, lhsT=lhsT, rhs=rhs, start=start, stop=stop), r, w, inc=inc)

    def tp(out, in_, ident, r, w, inc=True):
        S.op("pe", lambda e: e.transpose(out=out, in_=in_, identity=ident), r, w, inc=inc)

    def dma(q, out, in_, sb, r, w):
        S.dma(q, lambda e: e.dma_start(out=out, in_=in_), sb, r, w)

    def bc(ap2, n):
        return ap2.unsqueeze(2).to_broadcast([128, ap2.shape[1], n])

    def bc1(ap2, k):
        return ap2.unsqueeze(1).to_broadcast([128, k, ap2.shape[1]])

    ipA = ps("ipA", [128, 512], F32); b_ipA = Buf("ipA")
    ipB = ps("ipB", [128, 512], F32); b_ipB = Buf("ipB")
    pT = ps("pT", [128, 1024], BF16); b_pT = Buf("pT")
    CBs = ps("CBs", [128, 512], F32); b_CBs = Buf("CBs")
    YY = ps("YY", [128, 512], F32); b_YY = Buf("YY")
    SN = ps("SN", [128, 512], F32); b_SN = Buf("SN")
    ST = ps("ST", [128, 512], F32); b_ST = Buf("ST")
    PV = ps("PV", [128, 512], F32); b_PV = Buf("PV")
    ipbanks = [(ipA, b_ipA), (ipB, b_ipB), (SN, b_SN), (CBs, b_CBs), (YY, b_YY), (ST, b_ST), (PV, b_PV)]
    ipc = [0]

    ippool = [3]

    def next_ip():
        r = ipbanks[ipc[0] % ippool[0]]
        ipc[0] += 1
        return r

    cst = alloc([768], F32); b_cst = Buf("cst")
    ident_f = cst[:, 0:128]; Ubd = cst[:, 128:256]; Tincl = cst[:, 256:384]
    T2 = cst[:, 384:448]; tri = cst[:, 448:512]; csel = [cst[:, 512:640], cst[:, 640:768]]
    identb = alloc([128], BF16); b_identb = Buf("identb")
    pfm = alloc([192], F32); b_pfm = Buf("pfm")
    nwpre = pfm[:, 0:16]; ssdnw = pfm[:, 16:32]
    convw = pfm[:, 32:160].rearrange("p (a b) -> p a b", a=32); convb = pfm[:, 160:192]
    prow = alloc([2160], F32); b_prow = Buf("prow")
    dtb_r = prow[:, 0:32]; a_r = prow[:, 32:64]; dskip_r = prow[:, 64:96]; esink_r = prow[:, 96:112]
    nwpost = prow[:, 112:2160]
    kb = alloc([NT], F32); b_kb = Buf("kb")
    carry = alloc([32, 3], F32); b_carry = [Buf("carry%d" % i) for i in range(32)]
    state = alloc([8, 256], F32); b_state = [Buf("state%d" % i) for i in range(8)]
    kT = [[alloc([128], BF16) for _ in range(2)] for _ in range(4)]
    b_kT = [[Buf("kT%d_%d" % (k, s)) for s in range(2)] for k in range(4)]
    vaug = [[alloc([65], BF16) for _ in range(2)] for _ in range(4)]
    b_vaug = [[Buf("va%d_%d" % (k, s)) for s in range(2)] for k in range(4)]
    mixT_off = (state_off[0] + 63) // 64 * 64
    mixT = alloc([24, NTBO * 128], BF16); b_mixT = [Buf("mixT%d" % i) for i in range(NTBO)]
    _save = state_off[0]
    state_off[0] = mixT_off
    NTBP = max(pre_blocks) if pre_blocks else 1
    xsB_all = alloc([NTBP, 384], BF16); b_xsBall = Buf("xsBall")
    xdW_all = alloc([NTBP, 256], BF16); b_xdW = Buf("xdW")
    lsum = alloc([2 * NTBP, 32], F32)
    Wlog = alloc([2 * NTBP, 32], F32)
    Wfac = alloc([2 * NTBP, 32], F32)
    Ffac = alloc([NTBP, 32], F32)
    Dtot = alloc([32], F32)
    b_pfx = Buf("pfx")
    assert state_off[0] <= _save
    state_off[0] = _save
    tokb = alloc([NTBA, 66], F32); b_tokb = Buf("tokb")
    NX = 1
    xbuf = [alloc([D], F32) for _ in range(NX)]; b_xbuf = [Buf("xbuf%d" % i) for i in range(NX)]
    stt_ = [alloc([8], F32) for _ in range(NX)]; b_stt = [Buf("st%d" % i) for i in range(NX)]
    epsc = alloc([8], F32); b_epsc = Buf("epsc")
    negb = alloc([32], F32); b_negb = Buf("negb")
    mark = state_off[0]

    hnT = alloc([16, NTBA * 128], BF16); b_hnT = [Buf("hnT%d" % i) for i in range(NTBA)]
    hnb = alloc([D], BF16); b_hnb = Buf("hnb")
    NSLOT = 4
    ring = [alloc([4096], BF16) for _ in range(NSLOT)]; b_ring = [Buf("ring%d" % i) for i in range(NSLOT)]
    aq = alloc([16, 256], BF16); b_aq = Buf("aq")
    akv = alloc([16, 128], BF16); b_akv = Buf("akv")
    ag = alloc([16, 256], BF16); b_ag = Buf("ag")
    wdt = alloc([16, 32], BF16); b_wdt = Buf("wdt")
    ubuf = [alloc([516], BF16) for _ in range(2)]; b_u = [Buf("u0"), Buf("u1")]
    dgb = [alloc([4, 128], BF16) for _ in range(2)]; b_dg = [Buf("dg0"), Buf("dg1")]
    silt = [alloc([512], F32) for _ in range(2)]; b_silt = [Buf("silt0"), Buf("silt1")]
    fmT_ = [alloc([4, NTBA * 128], BF16) for _ in range(2)]
    b_fmT_ = [[[Buf("fmT%d_%d_%d" % (q, c, i)) for i in range(NTBA)] for c in range(4)] for q in range(2)]
    dtall = alloc([NTBA, 224], F32); b_dt = [Buf("dt%d" % i) for i in range(NTBA)]
    dtmp = alloc([6, 32], F32); b_dtmp = Buf("dtmp")
    def dup(shape, dt, name):
        return [alloc(shape, dt) for _ in range(2)], [Buf(name + "0"), Buf(name + "1")]
    xsB_, b_xsB_ = dup([384], BF16, "xsB")
    sz_, b_sz_ = dup([256], F32, "sz")
    mCB_, b_mCB_ = dup([64], F32, "mCB")
    Rt_, b_R_ = dup([256], F32, "R")
    Et_, b_E_ = dup([256], F32, "E")
    MT_, b_MT_ = dup([4, 64], BF16, "MT")
    xdt_, b_xdt_ = dup([256], BF16, "xdt")
    xdte_, b_xdte_ = dup([256], BF16, "xdte")
    Sb_, b_Sb_ = dup([256], BF16, "Sb")
    t1_, b_t1_ = dup([256], F32, "t1")
    t2_, b_t2_ = dup([256], F32, "t2")
    yfin_, b_yfin_ = dup([256], BF16, "yfin")
    gn_, b_gn_ = dup([8], F32, "gn")
    YYs = [(YY, b_YY), (CBs, b_CBs)]
    qkf = alloc([320], F32); b_qkf = Buf("qkf")
    sg = alloc([256], F32); b_sg = Buf("sg")
    ra = alloc([160], F32); b_ra = Buf("ra")
    rb = alloc([160], F32); b_rb = Buf("rb")
    qkb = alloc([320], BF16); b_qkb = Buf("qkb")
    qT = alloc([4, 128], BF16); b_qT = Buf("qT")
    PT = alloc([512], BF16); b_PT = Buf("PT")
    den = alloc([8], F32); b_den = Buf("den")
    att = alloc([256], F32); b_att = Buf("att")
    attb = alloc([256], BF16); b_attb = Buf("attb")
    endA = state_off[0]
    state_off[0] = mark
    wout = alloc([24, D], BF16); b_wout = [Buf("wout%d" % i) for i in range(4)]
    junk = alloc([512], F32); b_junk = Buf("junk")
    otmp = alloc([512], F32); b_otmp = Buf("otmp")
    endB = state_off[0]
    state_off[0] = max(endA, endB)

    dma("sp", cst, cst_d, b_cst, [], [b_cst])
    dma("sp", pfm, pfm_d, b_pfm, [], [b_pfm])
    dma("sp", prow, prow_d.partition_broadcast(128), b_prow, [], [b_prow])
    dma("sp", kb, kball, b_kb, [], [b_kb])
    cp(identb, ident_f, [b_cst], [b_identb])
    act(a_r, a_r, AF.Exp, [b_prow], [b_prow])
    ts(a_r, a_r, -1.0, None, ALU.mult, None, [b_prow], [b_prow])
    act(esink_r, esink_r, AF.Exp, [b_prow], [b_prow])
    S.op("dve", lambda e: e.memset(epsc, EPS), [], [b_epsc])
    ts(negb, convb, -1.0, None, ALU.mult, None, [b_pfm], [b_negb])
    S.op("dve", lambda e: e.memset(carry, 0.0), [], b_carry)
    S.op("dve", lambda e: e.memset(state, 0.0), [], b_state)
    for k in range(4):
        for s in range(2):
            S.op("dve", lambda e, k=k, s=s: e.memset(vaug[k][s], 1.0), [], [b_vaug[k][s]])
            S.op("dve", lambda e, k=k, s=s: e.memset(kT[k][s], 0.0), [], [b_kT[k][s]])

    ringc = [0]

    def load_w(pieces):
        s = ringc[0] % NSLOT
        ringc[0] += 1
        ntot = sum(p[1] for p in pieces)
        view = ring[s][:, 0:16 * ntot].rearrange("p (k c) -> p k c", k=16)
        o = 0
        for (c0, n) in pieces:
            dma("pool", view[:, :, o:o + n], w_in_v[:, :, c0:c0 + n], b_ring[s], [], [b_ring[s]])
            o += n
        return view, b_ring[s]

    xc = [0]

    def stage0(i, t):
        s = xc[0] % NX
        xc[0] += 1
        xb_, bx = xbuf[s], b_xbuf[s]
        st, bst = stt_[s], b_stt[s]
        dma("sp", xb_, xin[t * 128:(t + 1) * 128, :], bx, [], [bx])
        act(hnb, xb_, AF.Square, [bx], [b_hnb, bst], accum=st[:, 0:1])
        rstd_of(st[:, 3:4], st[:, 2:3], st[:, 0:1], D, bst)
        ts(hnb, xb_, st[:, 3:4], None, ALU.mult, None, [bx, bst], [b_hnb])
        for h in range(2):
            for j in range(8):
                kc = h * 8 + j
                tp(pT[:, j * 128:(j + 1) * 128], hnb[:, kc * 128:(kc + 1) * 128], identb,
                   [b_hnb, b_identb], [b_pT], inc=(j == 7))
            tt(hnT[:, h * 8:(h + 1) * 8, i * 128:(i + 1) * 128],
               pT[:, 0:1024].rearrange("p (a b) -> p a b", a=8),
               bc(nwpre[:, h * 8:(h + 1) * 8], 128), ALU.mult, [b_pT, b_pfm], [b_hnT[i]])

    def stage1(i, pre=False):
        ip, bip = next_ip()
        for kc in range(16):
            mm(ip[:, 0:32], hnT[:, kc, i * 128:(i + 1) * 128], wdt[:, kc, :], kc == 0, kc == 15,
               [b_hnT[i], b_wdt], [bip], inc=(kc == 15))
        d = dtall[:, i, :]
        bd = b_dt[i]
        xr, ab, ee, ll, mx = dtmp[:, 0, :], dtmp[:, 1, :], dtmp[:, 2, :], dtmp[:, 3, :], dtmp[:, 4, :]
        tt(xr, ip[:, 0:32], dtb_r, ALU.add, [bip, b_prow], [b_dtmp])
        ts(ab, xr, -1.0, None, ALU.mult, None, [b_dtmp], [b_dtmp])
        tt(ab, ab, xr, ALU.min, [b_dtmp], [b_dtmp])
        act(ee, ab, AF.Exp, [b_dtmp], [b_dtmp])
        act(ll, ee, AF.Ln, [b_dtmp], [b_dtmp], bias=1.0)
        ts(mx, xr, 0.0, None, ALU.max, None, [b_dtmp], [b_dtmp])
        tt(mx, mx, ll, ALU.add, [b_dtmp], [b_dtmp])
        ts(d[:, 0:32], mx, tokb[:, i, 0:1], None, ALU.mult, None, [b_dtmp, b_tokb], [bd])
        tt(d[:, 32:64], d[:, 0:32], a_r, ALU.mult, [bd, b_prow], [bd])
        ip2, bip2 = next_ip()
        mm(ip2[:, 0:32], Tincl, d[:, 32:64], True, True, [b_cst, bd], [bip2], inc=False)
        mm(ip2[:, 32:64], Ubd, d[:, 32:64], True, True, [b_cst, bd], [bip2], inc=False)
        mm(ip2[:, 64:96], csel[0], d[:, 32:64], True, True, [b_cst, bd], [bip2], inc=False)
        mm(ip2[:, 96:128], csel[1], d[:, 32:64], True, True, [b_cst, bd], [bip2], inc=True)
        act(d[:, 64:192], ip2[:, 0:128], AF.Exp, [bip2], [bd])
        if pre and os.environ.get('K_NOLS', '0') != '1':
            act(lsum[:, 2 * i:2 * i + 2, :], ip2[:, 64:128].rearrange("p (a b) -> p a b", a=2), AF.Copy, [bip2], [b_pfx])
        tt(d[:, 192:224], d[:, 0:32], d[:, 96:128], ALU.mult, [bd], [bd])

    uc = [0]

    dgc = [0]

    def conv_sub(wview, bw, wc0, ch, fct, t0, n, dslot, gb):
        tl = list(range(t0 // 128, (t0 + n) // 128))
        fmT, b_fmT = fmT_[gb], b_fmT_[gb]
        dg, bdg = dgb[dslot], b_dg[dslot]
        if t0 == 0:
            tt(dg, bc1(identb, 4), bc(convw[:, ch, 0:4], 128), ALU.mult, [b_identb, b_pfm], [bdg])
        ip, bip = next_ip()
        for kc in range(16):
            mm(ip[:, 0:n], wview[:, kc, wc0:wc0 + 128], hnT[:, kc, t0:t0 + n], kc == 0, kc == 15,
               [bw] + [b_hnT[x] for x in tl], [bip], inc=(kc == 15))
        s = uc[0] % 2
        uc[0] += 1
        u, bu = ubuf[s], b_u[s]
        act(u[:, 3:3 + n], ip[:, 0:n], AF.Copy, [bip], [bu])
        cp(u[:, 0:3], carry[:, ch, :], [b_carry[ch]], [bu])
        cp(carry[:, ch, :], u[:, n:n + 3], [bu], [b_carry[ch]])
        yield
        cps, bcps = next_ip()
        for k in range(4):
            mm(cps[:, 0:n], dg[:, k, :], u[:, k:k + n], k == 0, k == 3, [bdg, bu], [bcps], inc=(k == 3))
        tmp, btmp = silt[s][:, 0:n], b_silt[s]
        act(tmp, cps[:, 0:n], AF.Exp, [bcps, b_negb], [btmp], scale=-1.0, bias=negb[:, ch:ch + 1])
        act(tmp, tmp, AF.Ln, [btmp], [btmp], bias=1.0)
        act(tmp, tmp, AF.Exp, [btmp], [btmp], scale=-1.0)
        stt(fmT[:, fct, t0:t0 + n], cps[:, 0:n], convb[:, ch:ch + 1], tmp, ALU.add, ALU.mult, [bcps, b_pfm, btmp],
            [b_fmT[fct][x] for x in tl])
        yield

    def conv_pipe(facts):
        n = len(facts)
        g = [None] * n

        def stepA(k):
            g[k] = facts[k]()
            next(g[k])

        stepA(0)
        yield
        if n > 1:
            stepA(1)
            yield
        for k in range(n):
            next(g[k])
            if k + 2 < n:
                stepA(k + 2)
            yield

    def conv_facts(wview, bw, wc0, ch, fct, tiles_n, gb):
        ntok = tiles_n * 128
        out = []
        t0 = 0
        dslot = dgc[0] % 2
        dgc[0] += 1
        while t0 < ntok:
            n = min(512, ntok - t0)
            out.append(lambda t0=t0, n=n: conv_sub(wview, bw, wc0, ch, fct, t0, n, dslot, gb))
            t0 += n
        return out

    def ssd_tile(i, g, full, wz=None, bwz=None):
        par = i % 2
        fmT, b_fmT = fmT_[g % 2], b_fmT_[g % 2]
        xsB, b_xsB = xsB_[par], b_xsB_[par]
        sz, b_sz = sz_[par], b_sz_[par]
        mCB, b_mCB = mCB_[par], b_mCB_[par]
        Rt, b_R = Rt_[par], b_R_[par]
        Et, b_E = Et_[par], b_E_[par]
        MT, b_MT = MT_[par], b_MT_[par]
        xdt, b_xdt = xdt_[par], b_xdt_[par]
        xdte, b_xdte = xdte_[par], b_xdte_[par]
        Sb, b_Sb = Sb_[par], b_Sb_[par]
        t1, b_t1 = t1_[par], b_t1_[par]
        t2, b_t2 = t2_[par], b_t2_[par]
        yfin, b_yfin = yfin_[par], b_yfin_[par]
        gn, b_gn = gn_[par], b_gn_[par]
        Yp, b_Yp = YYs[par]
        tok = slice(i * 128, (i + 1) * 128)
        d = dtall[:, i, :]
        bd = b_dt[i]
        hs = slice(4 * g, 4 * g + 4)
        if full:
            ip, bip = next_ip()
            for kc in range(16):
                mm(ip[:, 0:256], hnT[:, kc, tok], wz[:, kc, :], kc == 0, kc == 15, [b_hnT[i], bwz], [bip], inc=(kc == 15))
            silu_mul(sz, ip[:, 0:256], sz, [bip], b_sz, [b_sz])
            yield
        for j in range(3):
            tp(pT[:, j * 128:(j + 1) * 128], fmT[:, j, tok], identb, [b_fmT[j][i], b_identb], [b_pT], inc=(j == 2))
        cp(xsB, pT[:, 0:384], [b_pT], [b_xsB])
        yield
        xs4 = xsB[:, 0:256].rearrange("p (a b) -> p a b", a=4)
        tt(xdte.rearrange("p (a b) -> p a b", a=4), xs4, bc(d[:, 192 + 4 * g:196 + 4 * g], 64), ALU.mult, [b_xsB, bd], [b_xdte])
        if full:
            cb, bcb = next_ip()
            mm(cb[:, 0:128], fmT[:, 2, tok], fmT[:, 3, tok], True, True, [b_fmT[2][i], b_fmT[3][i]], [bcb])
            tt(mCB[0:64, :], cb[0:64, 0:64], tri[0:64, :], ALU.mult, [bcb, b_cst], [b_mCB])
            tt(mCB[64:128, :], cb[64:128, 64:128], tri[64:128, :], ALU.mult, [bcb, b_cst], [b_mCB])
            tt(Rt.rearrange("p (a b) -> p a b", a=4), bc1(T2, 4), bc(d[:, 32 + 4 * g:36 + 4 * g], 64), ALU.mult, [b_cst, bd], [b_R])
            yield
            mm(cb[:, 128:384], Ubd, Rt, True, True, [b_cst, b_R], [bcb])
            act(Et, cb[:, 128:384], AF.Exp, [bcb], [b_E])
            tt(MT, Et.rearrange("p (a b) -> p a b", a=4), bc1(mCB, 4), ALU.mult, [b_E, b_mCB], [b_MT])
            tt(xdt.rearrange("p (a b) -> p a b", a=4), xs4, bc(d[:, hs], 64), ALU.mult, [b_xsB, bd], [b_xdt])
            yield
            for c in range(2):
                pr = slice(64 * c, 64 * c + 64)
                for h in range(4):
                    mm(Yp[pr, 64 * h:64 * h + 64], MT[pr, h, :], xdt[pr, 64 * h:64 * h + 64], True, True,
                       [b_MT, b_xdt], [b_Yp], inc=(c == 1 and h == 3))
            yield
        for c in range(2):
            pr = slice(64 * c, 64 * c + 64)
            sn, bsn = next_ip()
            mm(sn[:, 0:256], xsB[pr, 256:384], xdte[pr, :], True, True, [b_xsB, b_xdte], [bsn])
            if full:
                act(Sb, state[:, g, :], AF.Copy, [b_state[g]], [b_Sb])
                mm(Yp[pr, 256:512], fmT[:, 3, i * 128 + 64 * c:i * 128 + 64 * c + 64], Sb, True, True,
                   [b_fmT[3][i], b_Sb], [b_Yp])
            st4 = state[:, g, :].rearrange("p (a b) -> p a b", a=4)
            tt(st4, st4, bc(d[:, 128 + 32 * c + 4 * g:132 + 32 * c + 4 * g], 64), ALU.mult, [b_state[g], bd], [b_state[g]])
            tt(state[:, g, :], state[:, g, :], sn[:, 0:256], ALU.add, [b_state[g], bsn], [b_state[g]])
            yield
        if not full:
            return
        t14 = t1.rearrange("p (a b) -> p a b", a=4)
        tt(t2.rearrange("p (a b) -> p a b", a=4), xs4, bc(dskip_r[:, hs], 64), ALU.mult, [b_xsB, b_prow], [b_t2])
        tt(t14, Yp[:, 256:512].rearrange("p (a b) -> p a b", a=4), bc(d[:, 64 + 4 * g:68 + 4 * g], 64), ALU.mult, [b_Yp, bd], [b_t1])
        tt(t1, t1, Yp[:, 0:256], ALU.add, [b_t1, b_Yp], [b_t1])
        yield
        tt(t1, t1, t2, ALU.add, [b_t1, b_t2], [b_t1])
        tt(t1, t1, sz, ALU.mult, [b_t1, b_sz], [b_t1])
        act(t2, t1, AF.Square, [b_t1], [b_t2, b_gn], accum=gn[:, 0:1])
        yield
        rstd_of(gn[:, 3:4], gn[:, 2:3], gn[:, 0:1], 256, b_gn)
        ts(yfin, t1, gn[:, 3:4], None, ALU.mult, None, [b_t1, b_gn], [b_yfin])
        yield
        for j in range(2):
            tp(pT[:, j * 128:(j + 1) * 128], yfin[:, j * 128:(j + 1) * 128], identb, [b_yfin, b_identb], [b_pT], inc=(j == 1))
        tt(mixT[:, 2 * g:2 * g + 2, tok], pT[:, 0:256].rearrange("p (a b) -> p a b", a=2),
           bc(ssdnw[:, 2 * g:2 * g + 2], 128), ALU.mult, [b_pT, b_pfm], [b_mixT[i]])
        yield

    def prefix_factors(n):
        nc2 = 2 * n
        S.op("dve", lambda e: e.memset(Wlog[:, nc2 - 1, :], 0.0), [], [b_pfx])
        for c in range(nc2 - 2, -1, -1):
            tt(Wlog[:, c, :], Wlog[:, c + 1, :], lsum[:, c + 1, :], ALU.add, [b_pfx], [b_pfx])
        tt(Dtot, Wlog[:, 0, :], lsum[:, 0, :], ALU.add, [b_pfx], [b_pfx])
        act(Wfac[:, 0:nc2, :], Wlog[:, 0:nc2, :], AF.Exp, [b_pfx], [b_pfx])
        act(Dtot, Dtot, AF.Exp, [b_pfx], [b_pfx])
        for i in range(n):
            for cc in range(2):
                pr = slice(64 * cc, 64 * cc + 64)
                tt(Ffac[pr, i, :], dtall[pr, i, 192:224], Wfac[pr, 2 * i + cc, :], ALU.mult, [b_dt[i], b_pfx], [b_pfx])

    def ssd_prefix_group(g, n):
        fmT, b_fmT = fmT_[g % 2], b_fmT_[g % 2]
        for i0 in range(0, n, 2):
            m = min(2, n - i0)
            for ii in range(m):
                for j in range(3):
                    col = (ii * 3 + j) * 128
                    tp(pT[:, col:col + 128], fmT[:, j, (i0 + ii) * 128:(i0 + ii + 1) * 128], identb,
                       [b_fmT[j][i0 + ii], b_identb], [b_pT], inc=(ii == m - 1 and j == 2))
            cp(xsB_all[:, i0:i0 + m, :], pT[:, 0:m * 384].rearrange("p (a b) -> p a b", a=m), [b_pT], [b_xsBall])
            yield
        tt(xdW_all[:, 0:n, :].rearrange("p n (a b) -> p n a b", a=4),
           xsB_all[:, 0:n, 0:256].rearrange("p n (a b) -> p n a b", a=4),
           Ffac[:, 0:n, 4 * g:4 * g + 4].unsqueeze(3).to_broadcast([128, n, 4, 64]), ALU.mult, [b_xsBall, b_pfx], [b_xdW])
        sn, bsn = next_ip()
        for i in range(n):
            mm(sn[:, 0:256], xsB_all[:, i, 256:384], xdW_all[:, i, :], i == 0, i == n - 1, [b_xsBall, b_xdW], [bsn], inc=(i == n - 1))
        st4 = state[:, g, :].rearrange("p (a b) -> p a b", a=4)
        tt(st4, st4, bc(Dtot[:, 4 * g:4 * g + 4], 64), ALU.mult, [b_state[g], b_pfx], [b_state[g]])
        tt(state[:, g, :], state[:, g, :], sn[:, 0:256], ALU.add, [b_state[g], bsn], [b_state[g]])
        yield

    def lagged(factories, lag=2):
        active = []
        idx = 0
        while idx < len(factories) or active:
            if idx < len(factories) and len(active) < 2 and (not active or active[-1][1] >= lag):
                active.append([factories[idx](), 0])
                idx += 1
            for a in list(active):
                try:
                    next(a[0])
                    a[1] += 1
                except StopIteration:
                    active.remove(a)
            yield

    def attn_tile(i, t, kv, kv_only):
        tok = slice(i * 128, (i + 1) * 128)
        cur, prv = t % 2, (t + 1) % 2
        ip, bip = next_ip()
        if not kv_only:
            for kc in range(16):
                mm(ip[:, 0:256], hnT[:, kc, tok], aq[:, kc, :], kc == 0, kc == 15, [b_hnT[i], b_aq], [bip], inc=False)
        for kc in range(16):
            mm(ip[:, 256:384], hnT[:, kc, tok], akv[:, kc, :], kc == 0, kc == 15, [b_hnT[i], b_akv], [bip], inc=(kc == 15))
        if kv_only:
            act(qkf[:, 256:320], ip[:, 256:320], AF.Copy, [bip], [b_qkf])
        else:
            act(qkf, ip[:, 0:320], AF.Copy, [bip], [b_qkf])
        act(vaug[kv][cur][:, 0:64], ip[:, 320:384], AF.Copy, [bip], [b_vaug[kv][cur]])
        yield
        if not kv_only:
            ip2, bip2 = next_ip()
            for kc in range(16):
                mm(ip2[:, 0:256], hnT[:, kc, tok], ag[:, kc, :], kc == 0, kc == 15, [b_hnT[i], b_ag], [bip2], inc=(kc == 15))
            silu_mul(sg, ip2[:, 0:256], sg, [bip2], b_sg, [b_sg])
            yield
        q4 = qkf.rearrange("p (h two f) -> p h two f", h=5, two=2)
        o4 = qkb.rearrange("p (h two f) -> p h two f", h=5, two=2)
        hsl = slice(4, 5) if kv_only else slice(0, 5)
        nh = 1 if kv_only else 5
        q4 = q4[:, hsl]
        o4 = o4[:, hsl]
        q1, q2 = q4[:, :, 0, :], q4[:, :, 1, :]
        cosb = bc1(tokb[:, i, 2:34], nh)
        sinb = bc1(tokb[:, i, 34:66], nh)
        ra3 = ra.rearrange("p (h f) -> p h f", h=5)[:, hsl]
        rb3 = rb.rearrange("p (h f) -> p h f", h=5)[:, hsl]
        tt(ra3, q1, cosb, ALU.mult, [b_qkf, b_tokb], [b_ra])
        tt(rb3, q2, sinb, ALU.mult, [b_qkf, b_tokb], [b_rb])
        tt(o4[:, :, 0, :], ra3, rb3, ALU.subtract, [b_ra, b_rb], [b_qkb])
        tt(ra3, q2, cosb, ALU.mult, [b_qkf, b_tokb], [b_ra])
        tt(rb3, q1, sinb, ALU.mult, [b_qkf, b_tokb], [b_rb])
        tt(o4[:, :, 1, :], ra3, rb3, ALU.add, [b_ra, b_rb], [b_qkb])
        js = [4] if kv_only else [0, 1, 2, 3, 4]
        for j in js:
            tp(pT[0:64, j * 128:(j + 1) * 128], qkb[:, j * 64:(j + 1) * 64], identb, [b_qkb, b_identb], [b_pT], inc=(j == 4))
        act(kT[kv][cur][0:64, :], pT[0:64, 512:640], AF.Copy, [b_pT], [b_kT[kv][cur]])
        if kv_only:
            yield
            return
        act(qT[0:64, :, :], pT[0:64, 0:512].rearrange("p (a b) -> p a b", a=4), AF.Copy, [b_pT], [b_qT])
        yield
        for c in range(2):
            pr = slice(64 * c, 64 * c + 64)
            if c == 0:
                kA, vA, bkA, bvA, tA = kT[kv][prv], vaug[kv][prv], b_kT[kv][prv], b_vaug[kv][prv], t - 1
                kB, vB, bkB, bvB, tB = kT[kv][cur], vaug[kv][cur], b_kT[kv][cur], b_vaug[kv][cur], t
                pB = slice(0, 64)
            else:
                kA, vA, bkA, bvA, tA = kT[kv][cur], vaug[kv][cur], b_kT[kv][cur], b_vaug[kv][cur], t
                kB, vB, bkB, bvB, tB = kT[kv][prv], vaug[kv][prv], b_kT[kv][prv], b_vaug[kv][prv], t - 1
                pB = slice(64, 128)
            qc = qT[0:64, :, 64 * c:64 * c + 64]
            mm(ST[:, 0:256], kA[0:64, :], qc, True, True, [bkA, b_qT], [b_ST], inc=False)
            mm(ST[pB, 256:512], kB[0:64, pB], qc, True, True, [bkB, b_qT], [b_ST])
            act(PT[:, 0:256], ST[:, 0:256], AF.Exp, [b_ST, b_kb], [b_PT], bias=kb[:, tA:tA + 1], scale=0.125)
            act(PT[pB, 256:512], ST[pB, 256:512], AF.Exp, [b_ST, b_kb], [b_PT], bias=kb[pB, tB:tB + 1], scale=0.125)
            yield
            for r in range(4):
                mm(PV[pr, r * 65:(r + 1) * 65], PT[:, r * 64:(r + 1) * 64], vA[:, :], True, False, [b_PT, bvA], [b_PV], inc=False)
                mm(PV[pr, r * 65:(r + 1) * 65], PT[pB, 256 + r * 64:256 + (r + 1) * 64], vB[pB, :], False, True,
                   [b_PT, bvB], [b_PV], inc=(r == 3))
            yield
        pv4 = PV[:, 0:260].rearrange("p (a b) -> p a b", a=4)
        tt(den[:, 0:4], pv4[:, :, 64], esink_r[:, 4 * kv:4 * kv + 4], ALU.add, [b_PV, b_prow], [b_den])
        recip(den[:, 0:4], den[:, 0:4], [b_den], [b_den])
        tt(att.rearrange("p (a b) -> p a b", a=4), pv4[:, :, 0:64], bc(den[:, 0:4], 64), ALU.mult, [b_PV, b_den], [b_att])
        tt(attb, att, sg, ALU.mult, [b_att, b_sg], [b_attb])
        for j in range(2):
            tp(pT[:, j * 128:(j + 1) * 128], attb[:, j * 128:(j + 1) * 128], identb, [b_attb, b_identb], [b_pT], inc=(j == 1))
        act(mixT[:, 16 + 2 * kv:18 + 2 * kv, tok], pT[:, 0:256].rearrange("p (a b) -> p a b", a=2), AF.Copy, [b_pT], [b_mixT[i]])
        yield

    obanks = [(CBs, b_CBs), (YY, b_YY), (SN, b_SN), (ST, b_ST)]

    def outproj_tile(i, t, orow):
        tok = slice(i * 128, (i + 1) * 128)
        s = xc[0] % NX
        xc[0] += 1
        xb_, bx = xbuf[s], b_xbuf[s]
        st, bst = stt_[s], b_stt[s]
        dma("sp", xb_, xin[t * 128:(t + 1) * 128, :], bx, [], [bx])
        for dc in range(4):
            ob, bob = obanks[dc]
            for m in range(24):
                mm(ob[:, 0:512], mixT[:, m, tok], wout[:, m, dc * 512:(dc + 1) * 512], m == 0, m == 23,
                   [b_mixT[i], b_wout[dc]], [bob], inc=(m == 23))
            act(junk, ob[:, 0:512], AF.Square, [bob], [b_junk, bst], accum=st[:, 4 + dc:5 + dc])
        S.op("dve", lambda e: e.reduce_sum(out=st[:, 0:1], in_=st[:, 4:8], axis=mybir.AxisListType.X), [bst], [bst])
        rstd_of(st[:, 3:4], st[:, 2:3], st[:, 0:1], D, bst)
        for dc in range(4):
            ob, bob = obanks[dc]
            cs = slice(dc * 512, (dc + 1) * 512)
            stt(otmp, ob[:, 0:512], st[:, 3:4], nwpost[:, cs], ALU.mult, ALU.mult, [bob, bst, b_prow], [b_otmp])
            tt(xb_[:, cs], xb_[:, cs], otmp, ALU.add, [bx, b_otmp], [bx])
        dma("sp", out_d[orow * 128:(orow + 1) * 128, :], xb_, bx, [bx], [])

    def interleave_gen(chains):
        live = [[g, w, 0] for g, w in chains]
        while live:
            live.sort(key=lambda c: c[2] / c[1])
            c = live[0]
            try:
                next(c[0])
                c[2] += 1
                yield
            except StopIteration:
                live.remove(c)

    def interleave(chains, key=None):
        live = [[g, w, 0, j] for j, (g, w) in enumerate(chains)]
        if step_counts is not None and key in step_counts:
            for c in live:
                c[1] = max(1, step_counts[key][c[3]])
        done = {}
        while live:
            live.sort(key=lambda c: c[2] / c[1])
            c = live[0]
            try:
                next(c[0])
                c[2] += 1
            except StopIteration:
                done[c[3]] = c[2]
                live.remove(c)
        counts_out[key] = [done[j] for j in range(len(chains))]

    def run_block(tiles, own, last_prefix, own_row0):
        n = len(tiles)
        t0 = tiles[0]
        dma("sp", tokb[:, 0:n, :], tokc[t0 * 128:(t0 + n) * 128, :].rearrange("(n p) c -> p n c", p=128), b_tokb, [], [b_tokb])
        dma("pool", wdt, w_in_v[:, :, OFF_DT:OFF_DT + 32], b_wdt, [], [b_wdt])
        for i, t in enumerate(tiles):
            stage0(i, t)
            if i >= 1:
                stage1(i - 1, not own)
        stage1(n - 1, not own)
        if not own and os.environ.get("K_NOPF", "0") != "1":
            prefix_factors(n)
        needC = own or last_prefix
        ippool[0] = 3 if (own or os.environ.get("K_POOL3", "0") == "1") else 7

        wzs = {}
        nsub_ = (n * 128 + 511) // 512
        w_conv = (4 if needC else 3) * nsub_ + 1
        w_ssd = (5 * n + 5) if own else ((n + 1) // 2 + 1)

        def group_conv(g):
            gb = g % 2
            wx, bwx = load_w([(OFF_XS + g * 256, 256)])
            pcs = [(OFF_B + g * 128, 128)] + ([(OFF_C + g * 128, 128)] if needC else [])
            wbc, bwbc = load_w(pcs)
            if own:
                wzs[g] = load_w([(OFF_Z + g * 256, 256)])
            facts = conv_facts(wx, bwx, 0, 2 * g, 0, n, gb) + conv_facts(wx, bwx, 128, 2 * g + 1, 1, n, gb)
            facts += conv_facts(wbc, bwbc, 0, 16 + g, 2, n, gb)
            if needC:
                facts += conv_facts(wbc, bwbc, 128, 24 + g, 3, n, gb)
            yield from conv_pipe(facts)

        def group_ssd(g):
            if own:
                wz, bwz = wzs[g]
                yield from lagged([(lambda i=i: ssd_tile(i, g, True, wz, bwz)) for i in range(n)])
            else:
                yield from ssd_prefix_group(g, n)

        def chainS():
            yield from group_conv(0)
            for g in range(8):
                subs = [(group_ssd(g), w_ssd)]
                if g < 7:
                    subs.append((group_conv(g + 1), w_conv))
                yield from interleave_gen(subs)

        def load_att(kv, kv_only):
            if not kv_only:
                dma("pool", aq, w_in_v[:, :, OFF_Q + kv * 256:OFF_Q + kv * 256 + 256], b_aq, [], [b_aq])
                dma("pool", ag, w_in_v[:, :, OFF_G + kv * 256:OFF_G + kv * 256 + 256], b_ag, [], [b_ag])
            dma("pool", akv[:, :, 0:64], w_in_v[:, :, OFF_K + kv * 64:OFF_K + kv * 64 + 64], b_akv, [], [b_akv])
            dma("pool", akv[:, :, 64:128], w_in_v[:, :, OFF_V + kv * 64:OFF_V + kv * 64 + 64], b_akv, [], [b_akv])

        def chainA():
            for kv in range(4):
                load_att(kv, False)
                for i, t in enumerate(tiles):
                    yield from attn_tile(i, t, kv, False)

        nsub = (n * 128 + 511) // 512
        if own:
            wS = w_conv + 8 * (w_ssd + w_conv)
            wA = 4 * n * 9
            interleave([(chainS(), wS), (chainA(), wA)], key=("own", tiles[0]))
            S.barrier()
            for dc in range(4):
                dma("pool", wout[:, :, dc * 512:(dc + 1) * 512], w_out_v[:, :, dc * 512:(dc + 1) * 512], b_wout[dc], [], [b_wout[dc]])
            for i, t in enumerate(tiles):
                outproj_tile(i, t, own_row0 + i)
            S.barrier()
        else:
            interleave([(chainS(), 1)], key=("pre", tiles[0]))
            if last_prefix:
                for kv in range(4):
                    load_att(kv, True)
                    for _ in attn_tile(n - 1, tiles[-1], kv, True):
                        pass

    t = 0
    for bi, nb in enumerate(pre_blocks):
        run_block(list(range(t, t + nb)), False, bi == len(pre_blocks) - 1, 0)
        t += nb
    if os.environ.get("K_NOBAR", "0") != "1":
        S.barrier()
    row0 = 0
    for nb in own_blocks:
        run_block(list(range(t, t + nb)), True, False, row0)
        t += nb
        row0 += nb
    S.final_wait("sp", b_xbuf)
    if step_counts is None:
        es.close()
        return counts_out
    S.emit(nc)
    es.close()
    return nc


def make_consts():
    k = np.arange(128)
    same = (k[:, None] // 64) == (k[None, :] // 64)
    c = np.zeros((128, 768), np.float32)
    c[:, 0:128] = np.eye(128)
    c[:, 128:256] = (same & (k[:, None] > k[None, :]))
    c[:, 256:384] = (same & (k[:, None] <= k[None, :]))
    l = np.arange(64)
    c[:, 384:448] = ((k[:, None] % 64) <= l[None, :])
    c[:, 448:512] = (l[None, :] >= (k[:, None] % 64))
    c[:, 512:640] = (k[:, None] < 64)
    c[:, 640:768] = (k[:, None] >= 64)
    return c


def kernel(x, meta_tokens, norm_pre_w, w_in, conv_w, conv_b, dt_bias, a_log, d_skip,
           ssd_norm_w, attn_sinks, w_out, norm_post_w, _O=None, _pre_max=6, _own_max=6, _same_sync=("dve", "act")):
    x = np.asarray(x, np.float32)
    bsz, seq, _ = x.shape
    nchunk = (PADL + NMETA + seq) // 64
    G = (nchunk + 1) // 2
    O = (G + 1) // 2 if _O is None else _O
    assert 2 * O - 1 >= G
    P = O - 1
    NT = P + O
    f32 = np.float32
    nslot = (2 * O - 1) * 128
    idx = np.arange(nslot)
    valid = ((idx >= PADL) & (idx < PADL + NMETA + seq)).astype(f32)
    pos = (idx - PADL).astype(f32)
    inv = (10000.0 ** (-np.arange(32, dtype=f32) / f32(32))).astype(f32)
    ang = (pos[:, None] * inv[None, :]).astype(f32)
    tokg = np.zeros((nslot, 66), f32)
    tokg[:, 0] = valid
    tokg[:, 1] = np.where(valid > 0, 0.0, NEG)
    tokg[:, 2:34] = np.cos(ang)
    tokg[:, 34:66] = np.sin(ang)
    meta = np.asarray(meta_tokens, f32)
    cst = make_consts()
    pfm = np.zeros((128, 192), f32)
    pfm[:, 0:16] = np.asarray(norm_pre_w, f32).reshape(16, 128).T
    pfm[:, 16:32] = np.asarray(ssd_norm_w, f32).reshape(16, 128).T
    cw = np.asarray(conv_w, f32).reshape(4, 32, 128)
    pfm[:, 32:160] = cw.transpose(2, 1, 0).reshape(128, 128)
    pfm[:, 160:192] = np.asarray(conv_b, f32).reshape(32, 128).T
    prow = np.concatenate([np.asarray(dt_bias, f32).reshape(-1), np.asarray(a_log, f32).reshape(-1),
                           np.asarray(d_skip, f32).reshape(-1), np.asarray(attn_sinks, f32).reshape(-1),
                           np.asarray(norm_post_w, f32).reshape(-1)]).reshape(1, 2160).astype(f32)
    w_in2 = np.ascontiguousarray(np.asarray(w_in, f32).reshape(D, DIN))
    w_out2 = np.ascontiguousarray(np.asarray(w_out, f32).reshape(DMIX, D))

    in_maps = []
    own_start = []
    for b in range(bsz):
        gx = np.zeros((nslot, D), f32)
        gx[PADL:PADL + NMETA] = meta
        gx[PADL + NMETA:PADL + NMETA + seq] = x[b]
        for j in range(2):
            if j == 0:
                xin = np.concatenate([np.zeros((P * 128, D), f32), gx[:O * 128]], 0)
                tk = np.zeros((NT * 128, 66), f32)
                tk[:P * 128, 1] = NEG
                tk[P * 128:] = tokg[:O * 128]
                own_start.append(0)
            else:
                xin = gx
                tk = tokg
                own_start.append(P * 128)
            kball = np.ascontiguousarray(tk[:, 1].reshape(NT, 128).T)
            in_maps.append({"xin": np.ascontiguousarray(xin), "tokc": np.ascontiguousarray(tk), "kball": kball,
                            "w_in": w_in2, "w_out": w_out2, "cst": cst, "pfm": pfm, "prow": prow})
    ncores = len(in_maps)
    counts = build_program(O, _pre_max, _own_max, _same_sync, None)
    nc = build_program(O, _pre_max, _own_max, _same_sync, counts)
    res = run_bass_kernel_spmd(nc, in_maps, core_ids=list(range(ncores)))
    out = np.zeros((bsz, seq, D), f32)
    for ci in range(ncores):
        b = ci // 2
        r = np.asarray(res.results[ci]["out"])
        s0 = own_start[ci]
        n0 = s0 - (PADL + NMETA)
        lo = max(0, -n0)
        hi = min(O * 128, seq - n0)
        out[b, n0 + lo:n0 + hi] = r[lo:hi]
    return out
```

```python
import os
import numpy as np
from contextlib import ExitStack
import concourse.bass as bass
import concourse.mybir as mybir
from concourse.bass_utils import run_bass_kernel_spmd

F32 = mybir.dt.float32
BF16 = mybir.dt.bfloat16
ALU = mybir.AluOpType
AF = mybir.ActivationFunctionType

D = 2048
DIN = 8736
DMIX = 3072
SEQ = 4096
NMETA = 16
PADL = 48
OFF_Z, OFF_XS, OFF_B, OFF_C, OFF_DT, OFF_Q, OFF_K, OFF_V, OFF_G = 0, 2048, 4096, 5120, 6144, 6176, 7200, 7456, 7712
EPS = 1e-6
NEG = -30000.0


class Buf:
    __slots__ = ("name", "last_w", "readers", "dma_cnt")

    def __init__(self, name):
        self.name = name
        self.last_w = None
        self.readers = {}
        self.dma_cnt = 0


class Sched:
    ENGS = ("pe", "act", "dve", "pool", "sp")

    def __init__(self, same_engine_sync=True):
        self.ops = {e: [] for e in self.ENGS}
        self.seq = {e: 0 for e in self.ENGS}
        self.waited = {e: {} for e in self.ENGS}
        self.semkeys = []
        self.cur = {}
        self.same = same_engine_sync

    def _need(self, eng, key, val):
        if key == eng and (eng == "pe" or (self.same is not True and eng not in self.same)):
            return
        if self.waited[eng].get(key, 0) >= val:
            return
        self.waited[eng][key] = val
        self.ops[eng].append(("wait", key, val))

    def _deps(self, eng, reads, writes):
        for b in reads:
            if b.last_w is not None:
                self._need(eng, *b.last_w)
        for b in writes:
            if b.last_w is not None and b.last_w[0] != eng:
                self._need(eng, *b.last_w)
            for k, v in b.readers.items():
                if k != eng:
                    self._need(eng, k, v)

    def _record(self, tok, reads, writes):
        for b in reads:
            if b.readers.get(tok[0], 0) < tok[1]:
                b.readers[tok[0]] = tok[1]
        for b in writes:
            b.last_w = tok
            b.readers = {}

    def op(self, eng, fn, reads=(), writes=(), inc=True, sreads=()):
        if sreads:
            sv = self.same
            self.same = True
            for b in sreads:
                if b.last_w is not None:
                    self._need(eng, *b.last_w)
            self.same = sv
        self._deps(eng, reads, writes)
        if inc:
            self.seq[eng] += 1
            tok = (eng, self.seq[eng])
        else:
            tok = (eng, self.seq[eng] + 1)
        if eng not in self.semkeys:
            self.semkeys.append(eng)
        self.cur[eng] = self.seq[eng]
        self._record(tok, reads, writes)
        self.ops[eng].append(("op", fn, eng if inc else None, 1))

    def dma(self, eng, fn, sb, reads=(), writes=()):
        self._deps(eng, reads, writes)
        key = "d_" + sb.name
        if key not in self.semkeys:
            self.semkeys.append(key)
        sb.dma_cnt += 1
        tok = (key, 16 * sb.dma_cnt)
        self.cur[key] = tok[1]
        self._record(tok, reads, writes)
        self.ops[eng].append(("op", fn, key, 16))

    def barrier(self):
        for e in self.ENGS:
            for k, v in self.cur.items():
                if v > 0:
                    self._need(e, k, v)

    def final_wait(self, eng, bufs):
        for b in bufs:
            if b.last_w is not None:
                self._need(eng, *b.last_w)
            for k, v in b.readers.items():
                self._need(eng, k, v)

    def emit(self, nc):
        with ExitStack() as es:
            sems = {}
            for i, k in enumerate(self.semkeys):
                sems[k] = es.enter_context(nc.semaphore("s%d_%s" % (i, k)))
            block = es.enter_context(nc.Block())

            def run(engname):
                def body(eng):
                    for o in self.ops[engname]:
                        if o[0] == "wait":
                            eng.wait_ge(sems[o[1]], o[2])
                        else:
                            ins = o[1](eng)
                            if o[2] is not None:
                                ins.then_inc(sems[o[2]], o[3])
                return body

            block.tensor(run("pe"))
            block.scalar(run("act"))
            block.vector(run("dve"))
            block.gpsimd(run("pool"))
            block.sync(run("sp"))


def split_blocks(n, mx):
    nb = (n + mx - 1) // mx
    base, rem = divmod(n, nb)
    return [base + (1 if i < rem else 0) for i in range(nb)]


def build_program(O, pre_max=8, own_max=6, same_sync=True, step_counts=None):
    counts_out = {}
    P = O - 1
    NT = P + O
    pre_blocks = split_blocks(P, pre_max) if P > 0 else []
    own_blocks = split_blocks(O, own_max)
    NTBA = max(pre_blocks + own_blocks)
    NTBO = max(own_blocks)

    nc = bass.Bass("TRN2", target_bir_lowering=False)
    xin = nc.dram_tensor("xin", [NT * 128, D], F32, kind="ExternalInput").ap()
    tokc = nc.dram_tensor("tokc", [NT * 128, 66], F32, kind="ExternalInput").ap()
    kball = nc.dram_tensor("kball", [128, NT], F32, kind="ExternalInput").ap()
    w_in = nc.dram_tensor("w_in", [D, DIN], F32, kind="ExternalInput").ap()
    w_out = nc.dram_tensor("w_out", [DMIX, D], F32, kind="ExternalInput").ap()
    cst_d = nc.dram_tensor("cst", [128, 768], F32, kind="ExternalInput").ap()
    pfm_d = nc.dram_tensor("pfm", [128, 192], F32, kind="ExternalInput").ap()
    prow_d = nc.dram_tensor("prow", [1, 2160], F32, kind="ExternalInput").ap()
    out_d = nc.dram_tensor("out", [O * 128, D], F32, kind="ExternalOutput").ap()

    w_in_v = w_in.rearrange("(kc p) c -> p kc c", p=128)
    w_out_v = w_out.rearrange("(m p) d -> p m d", p=128)

    S = Sched(same_engine_sync=same_sync)
    es = ExitStack()
    TOTAL = 212000
    arena = es.enter_context(nc.sbuf_tensor("arena", [128, TOTAL // 2], BF16))
    state_off = [0]

    def alloc(shape, dt):
        n = int(np.prod(shape))
        nbytes = n * (4 if dt == F32 else 2)
        o = (state_off[0] + 63) // 64 * 64
        state_off[0] = o + nbytes
        assert state_off[0] <= TOTAL, ("SBUF overflow", state_off[0])
        v = arena[:, o // 2:(o + nbytes) // 2]
        if dt == F32:
            v = v.bitcast(F32)
        if len(shape) == 2:
            v = v.rearrange("p (a b) -> p a b", a=shape[0])
        elif len(shape) == 3:
            v = v.rearrange("p (a b c) -> p a b c", a=shape[0], b=shape[1])
        return v

    def ps(name, shape, dt):
        return es.enter_context(nc.psum_tensor(name, shape, dt))

    def tt(out, in0, in1, op, r, w, eng="dve"):
        S.op(eng, lambda e: e.tensor_tensor(out=out, in0=in0, in1=in1, op=op), r, w)

    def ts(out, in0, s1, s2, op0, op1, r, w, eng="dve"):
        sr = r if not isinstance(s1, (int, float)) or not isinstance(s2, (int, float, type(None))) else ()
        if s2 is None:
            S.op(eng, lambda e: e.tensor_scalar(out=out, in0=in0, scalar1=s1, scalar2=None, op0=op0), r, w, sreads=sr)
        else:
            S.op(eng, lambda e: e.tensor_scalar(out=out, in0=in0, scalar1=s1, scalar2=s2, op0=op0, op1=op1), r, w, sreads=sr)

    def stt(out, in0, scalar, in1, op0, op1, r, w, eng="dve"):
        sr = r if not isinstance(scalar, (int, float)) else ()
        S.op(eng, lambda e: e.scalar_tensor_tensor(out=out, in0=in0, scalar=scalar, in1=in1, op0=op0, op1=op1), r, w, sreads=sr)

    def cp(out, in_, r, w, eng="dve"):
        S.op(eng, lambda e: e.tensor_copy(out=out, in_=in_), r, w)

    def recip(out, in_, r, w):
        S.op("dve", lambda e: e.reciprocal(out=out, in_=in_), r, w)

    def act(out, in_, func, r, w, bias=None, scale=None, accum=None):
        kw = {}
        if bias is not None:
            kw["bias"] = bias
        if scale is not None:
            kw["scale"] = scale
        if accum is not None:
            kw["accum_out"] = accum
        S.op("act", lambda e: e.activation(out=out, in_=in_, func=func, **kw), r, w)

    import os
    USE_SILU = os.environ.get("K_SILU", "0") == "1"

    def silu_mul(out, x, tmp, rx, btmp, wout_):
        if USE_SILU:
            act(out, x, AF.Silu, rx, wout_)
            return
        act(tmp, x, AF.Exp, rx, [btmp], scale=-1.0)
        act(tmp, tmp, AF.Ln, [btmp], [btmp], bias=1.0)
        act(tmp, tmp, AF.Exp, [btmp], [btmp], scale=-1.0)
        tt(out, x, tmp, ALU.mult, rx + [btmp], wout_)

    def rstd_of(dst, tmp, ssq, n, bst):
        act(tmp, ssq, AF.Ln, [bst, b_epsc], [bst], scale=1.0 / n, bias=epsc[:, 0:1])
        act(dst, tmp, AF.Exp, [bst], [bst], scale=-0.5)

    def mm(out, lhsT, rhs, start, stop, r, w, inc=True):
        S.op("pe", lambda e: e.matmul(out, lhsT=lhsT, rhs=rhs, start=start, stop=stop), r, w, inc=inc)

    def tp(out, in_, ident, r, w, inc=True):
        S.op("pe", lambda e: e.transpose(out=out, in_=in_, identity=ident), r, w, inc=inc)

    def dma(q, out, in_, sb, r, w):
        S.dma(q, lambda e: e.dma_start(out=out, in_=in_), sb, r, w)

    def bc(ap2, n):
        return ap2.unsqueeze(2).to_broadcast([128, ap2.shape[1], n])

    def bc1(ap2, k):
        return ap2.unsqueeze(1).to_broadcast([128, k, ap2.shape[1]])

    ipA = ps("ipA", [128, 512], F32); b_ipA = Buf("ipA")
    ipB = ps("ipB", [128, 512], F32); b_ipB = Buf("ipB")
    pT = ps("pT", [128, 1024], BF16); b_pT = Buf("pT")
    CBs = ps("CBs", [128, 512], F32); b_CBs = Buf("CBs")
    YY = ps("YY", [128, 512], F32); b_YY = Buf("YY")
    SN = ps("SN", [128, 512], F32); b_SN = Buf("SN")
    ST = ps("ST", [128, 512], F32); b_ST = Buf("ST")
    PV = ps("PV", [128, 512], F32); b_PV = Buf("PV")
    ipbanks = [(ipA, b_ipA), (ipB, b_ipB), (SN, b_SN), (CBs, b_CBs), (YY, b_YY), (ST, b_ST), (PV, b_PV)]
    ipc = [0]

    ippool = [3]

    def next_ip():
        r = ipbanks[ipc[0] % ippool[0]]
        ipc[0] += 1
        return r

    cst = alloc([768], F32); b_cst = Buf("cst")
    ident_f = cst[:, 0:128]; Ubd = cst[:, 128:256]; Tincl = cst[:, 256:384]
    T2 = cst[:, 384:448]; tri = cst[:, 448:512]; csel = [cst[:, 512:640], cst[:, 640:768]]
    identb = alloc([128], BF16); b_identb = Buf("identb")
    pfm = alloc([192], F32); b_pfm = Buf("pfm")
    nwpre = pfm[:, 0:16]; ssdnw = pfm[:, 16:32]
    convw = pfm[:, 32:160].rearrange("p (a b) -> p a b", a=32); convb = pfm[:, 160:192]
    prow = alloc([2160], F32); b_prow = Buf("prow")
    dtb_r = prow[:, 0:32]; a_r = prow[:, 32:64]; dskip_r = prow[:, 64:96]; esink_r = prow[:, 96:112]
    nwpost = prow[:, 112:2160]
    kb = alloc([NT], F32); b_kb = Buf("kb")
    carry = alloc([32, 3], F32); b_carry = [Buf("carry%d" % i) for i in range(32)]
    state = alloc([8, 256], F32); b_state = [Buf("state%d" % i) for i in range(8)]
    kT = [[alloc([128], BF16) for _ in range(2)] for _ in range(4)]
    b_kT = [[Buf("kT%d_%d" % (k, s)) for s in range(2)] for k in range(4)]
    vaug = [[alloc([65], BF16) for _ in range(2)] for _ in range(4)]
    b_vaug = [[Buf("va%d_%d" % (k, s)) for s in range(2)] for k in range(4)]
    mixT_off = (state_off[0] + 63) // 64 * 64
    mixT = alloc([24, NTBO * 128], BF16); b_mixT = [Buf("mixT%d" % i) for i in range(NTBO)]
    _save = state_off[0]
    state_off[0] = mixT_off
    NTBP = max(pre_blocks) if pre_blocks else 1
    xsB_all = alloc([NTBP, 384], BF16); b_xsBall = Buf("xsBall")
    xdW_all = alloc([NTBP, 256], BF16); b_xdW = Buf("xdW")
    lsum = alloc([2 * NTBP, 32], F32)
    Wlog = alloc([2 * NTBP, 32], F32)
    Wfac = alloc([2 * NTBP, 32], F32)
    Ffac = alloc([NTBP, 32], F32)
    Dtot = alloc([32], F32)
    b_pfx = Buf("pfx")
    assert state_off[0] <= _save
    state_off[0] = _save
    tokb = alloc([NTBA, 66], F32); b_tokb = Buf("tokb")
    NX = 1
    xbuf = [alloc([D], F32) for _ in range(NX)]; b_xbuf = [Buf("xbuf%d" % i) for i in range(NX)]
    stt_ = [alloc([8], F32) for _ in range(NX)]; b_stt = [Buf("st%d" % i) for i in range(NX)]
    epsc = alloc([8], F32); b_epsc = Buf("epsc")
    negb = alloc([32], F32); b_negb = Buf("negb")
    mark = state_off[0]

    hnT = alloc([16, NTBA * 128], BF16); b_hnT = [Buf("hnT%d" % i) for i in range(NTBA)]
    hnb = alloc([D], BF16); b_hnb = Buf("hnb")
    NSLOT = 4
    ring = [alloc([4096], BF16) for _ in range(NSLOT)]; b_ring = [Buf("ring%d" % i) for i in range(NSLOT)]
    aq = alloc([16, 256], BF16); b_aq = Buf("aq")
    akv = alloc([16, 128], BF16); b_akv = Buf("akv")
    ag = alloc([16, 256], BF16); b_ag = Buf("ag")
    wdt = alloc([16, 32], BF16); b_wdt = Buf("wdt")
    ubuf = [alloc([516], BF16) for _ in range(2)]; b_u = [Buf("u0"), Buf("u1")]
    dgb = [alloc([4, 128], BF16) for _ in range(2)]; b_dg = [Buf("dg0"), Buf("dg1")]
    silt = [alloc([512], F32) for _ in range(2)]; b_silt = [Buf("silt0"), Buf("silt1")]
    fmT_ = [alloc([4, NTBA * 128], BF16) for _ in range(2)]
    b_fmT_ = [[[Buf("fmT%d_%d_%d" % (q, c, i)) for i in range(NTBA)] for c in range(4)] for q in range(2)]
    dtall = alloc([NTBA, 224], F32); b_dt = [Buf("dt%d" % i) for i in range(NTBA)]
    dtmp = alloc([6, 32], F32); b_dtmp = Buf("dtmp")
    def dup(shape, dt, name):
        return [alloc(shape, dt) for _ in range(2)], [Buf(name + "0"), Buf(name + "1")]
    xsB_, b_xsB_ = dup([384], BF16, "xsB")
    sz_, b_sz_ = dup([256], F32, "sz")
    mCB_, b_mCB_ = dup([64], F32, "mCB")
    Rt_, b_R_ = dup([256], F32, "R")
    Et_, b_E_ = dup([256], F32, "E")
    MT_, b_MT_ = dup([4, 64], BF16, "MT")
    xdt_, b_xdt_ = dup([256], BF16, "xdt")
    xdte_, b_xdte_ = dup([256], BF16, "xdte")
    Sb_, b_Sb_ = dup([256], BF16, "Sb")
    t1_, b_t1_ = dup([256], F32, "t1")
    t2_, b_t2_ = dup([256], F32, "t2")
    yfin_, b_yfin_ = dup([256], BF16, "yfin")
    gn_, b_gn_ = dup([8], F32, "gn")
    YYs = [(YY, b_YY), (CBs, b_CBs)]
    qkf = alloc([320], F32); b_qkf = Buf("qkf")
    sg = alloc([256], F32); b_sg = Buf("sg")
    ra = alloc([160], F32); b_ra = Buf("ra")
    rb = alloc([160], F32); b_rb = Buf("rb")
    qkb = alloc([320], BF16); b_qkb = Buf("qkb")
    qT = alloc([4, 128], BF16); b_qT = Buf("qT")
    PT = alloc([512], BF16); b_PT = Buf("PT")
    den = alloc([8], F32); b_den = Buf("den")
    att = alloc([256], F32); b_att = Buf("att")
    attb = alloc([256], BF16); b_attb = Buf("attb")
    endA = state_off[0]
    state_off[0] = mark
    wout = alloc([24, D], BF16); b_wout = [Buf("wout%d" % i) for i in range(4)]
    junk = alloc([512], F32); b_junk = Buf("junk")
    otmp = alloc([512], F32); b_otmp = Buf("otmp")
    endB = state_off[0]
    state_off[0] = max(endA, endB)

    dma("sp", cst, cst_d, b_cst, [], [b_cst])
    dma("sp", pfm, pfm_d, b_pfm, [], [b_pfm])
    dma("sp", prow, prow_d.partition_broadcast(128), b_prow, [], [b_prow])
    dma("sp", kb, kball, b_kb, [], [b_kb])
    cp(identb, ident_f, [b_cst], [b_identb])
    act(a_r, a_r, AF.Exp, [b_prow], [b_prow])
    ts(a_r, a_r, -1.0, None, ALU.mult, None, [b_prow], [b_prow])
    act(esink_r, esink_r, AF.Exp, [b_prow], [b_prow])
    S.op("dve", lambda e: e.memset(epsc, EPS), [], [b_epsc])
    ts(negb, convb, -1.0, None, ALU.mult, None, [b_pfm], [b_negb])
    S.op("dve", lambda e: e.memset(carry, 0.0), [], b_carry)
    S.op("dve", lambda e: e.memset(state, 0.0), [], b_state)
    for k in range(4):
        for s in range(2):
            S.op("dve", lambda e, k=k, s=s: e.memset(vaug[k][s], 1.0), [], [b_vaug[k][s]])
            S.op("dve", lambda e, k=k, s=s: e.memset(kT[k][s], 0.0), [], [b_kT[k][s]])

    ringc = [0]

    def load_w(pieces):
        s = ringc[0] % NSLOT
        ringc[0] += 1
        ntot = sum(p[1] for p in pieces)
        view = ring[s][:, 0:16 * ntot].rearrange("p (k c) -> p k c", k=16)
        o = 0
        for (c0, n) in pieces:
            dma("pool", view[:, :, o:o + n], w_in_v[:, :, c0:c0 + n], b_ring[s], [], [b_ring[s]])
            o += n
        return view, b_ring[s]

    xc = [0]

    def stage0(i, t):
        s = xc[0] % NX
        xc[0] += 1
        xb_, bx = xbuf[s], b_xbuf[s]
        st, bst = stt_[s], b_stt[s]
        dma("sp", xb_, xin[t * 128:(t + 1) * 128, :], bx, [], [bx])
        act(hnb, xb_, AF.Square, [bx], [b_hnb, bst], accum=st[:, 0:1])
        rstd_of(st[:, 3:4], st[:, 2:3], st[:, 0:1], D, bst)
        ts(hnb, xb_, st[:, 3:4], None, ALU.mult, None, [bx, bst], [b_hnb])
        for h in range(2):
            for j in range(8):
                kc = h * 8 + j
                tp(pT[:, j * 128:(j + 1) * 128], hnb[:, kc * 128:(kc + 1) * 128], identb,
                   [b_hnb, b_identb], [b_pT], inc=(j == 7))
            tt(hnT[:, h * 8:(h + 1) * 8, i * 128:(i + 1) * 128],
               pT[:, 0:1024].rearrange("p (a b) -> p a b", a=8),
               bc(nwpre[:, h * 8:(h + 1) * 8], 128), ALU.mult, [b_pT, b_pfm], [b_hnT[i]])

    def stage1(i, pre=False):
        ip, bip = next_ip()
        for kc in range(16):
            mm(ip[:, 0:32], hnT[:, kc, i * 128:(i + 1) * 128], wdt[:, kc, :], kc == 0, kc == 15,
               [b_hnT[i], b_wdt], [bip], inc=(kc == 15))
        d = dtall[:, i, :]
        bd = b_dt[i]
        xr, ab, ee, ll, mx = dtmp[:, 0, :], dtmp[:, 1, :], dtmp[:, 2, :], dtmp[:, 3, :], dtmp[:, 4, :]
        tt(xr, ip[:, 0:32], dtb_r, ALU.add, [bip, b_prow], [b_dtmp])
        ts(ab, xr, -1.0, None, ALU.mult, None, [b_dtmp], [b_dtmp])
        tt(ab, ab, xr, ALU.min, [b_dtmp], [b_dtmp])
        act(ee, ab, AF.Exp, [b_dtmp], [b_dtmp])
        act(ll, ee, AF.Ln, [b_dtmp], [b_dtmp], bias=1.0)
        ts(mx, xr, 0.0, None, ALU.max, None, [b_dtmp], [b_dtmp])
        tt(mx, mx, ll, ALU.add, [b_dtmp], [b_dtmp])
        ts(d[:, 0:32], mx, tokb[:, i, 0:1], None, ALU.mult, None, [b_dtmp, b_tokb], [bd])
        tt(d[:, 32:64], d[:, 0:32], a_r, ALU.mult, [bd, b_prow], [bd])
        ip2, bip2 = next_ip()
        mm(ip2[:, 0:32], Tincl, d[:, 32:64], True, True, [b_cst, bd], [bip2], inc=False)
        mm(ip2[:, 32:64], Ubd, d[:, 32:64], True, True, [b_cst, bd], [bip2], inc=False)
        mm(ip2[:, 64:96], csel[0], d[:, 32:64], True, True, [b_cst, bd], [bip2], inc=False)
        mm(ip2[:, 96:128], csel[1], d[:, 32:64], True, True, [b_cst, bd], [bip2], inc=True)
        act(d[:, 64:192], ip2[:, 0:128], AF.Exp, [bip2], [bd])
        if pre and os.environ.get('K_NOLS', '0') != '1':
            act(lsum[:, 2 * i:2 * i + 2, :], ip2[:, 64:128].rearrange("p (a b) -> p a b", a=2), AF.Copy, [bip2], [b_pfx])
        tt(d[:, 192:224], d[:, 0:32], d[:, 96:128], ALU.mult, [bd], [bd])

    uc = [0]

    dgc = [0]

    def conv_sub(wview, bw, wc0, ch, fct, t0, n, dslot, gb):
        tl = list(range(t0 // 128, (t0 + n) // 128))
        fmT, b_fmT = fmT_[gb], b_fmT_[gb]
        dg, bdg = dgb[dslot], b_dg[dslot]
        if t0 == 0:
            tt(dg, bc1(identb, 4), bc(convw[:, ch, 0:4], 128), ALU.mult, [b_identb, b_pfm], [bdg])
        ip, bip = next_ip()
        for kc in range(16):
            mm(ip[:, 0:n], wview[:, kc, wc0:wc0 + 128], hnT[:, kc, t0:t0 + n], kc == 0, kc == 15,
               [bw] + [b_hnT[x] for x in tl], [bip], inc=(kc == 15))
        s = uc[0] % 2
        uc[0] += 1
        u, bu = ubuf[s], b_u[s]
        act(u[:, 3:3 + n], ip[:, 0:n], AF.Copy, [bip], [bu])
        cp(u[:, 0:3], carry[:, ch, :], [b_carry[ch]], [bu])
        cp(carry[:, ch, :], u[:, n:n + 3], [bu], [b_carry[ch]])
        yield
        cps, bcps = next_ip()
        for k in range(4):
            mm(cps[:, 0:n], dg[:, k, :], u[:, k:k + n], k == 0, k == 3, [bdg, bu], [bcps], inc=(k == 3))
        tmp, btmp = silt[s][:, 0:n], b_silt[s]
        act(tmp, cps[:, 0:n], AF.Exp, [bcps, b_negb], [btmp], scale=-1.0, bias=negb[:, ch:ch + 1])
        act(tmp, tmp, AF.Ln, [btmp], [btmp], bias=1.0)
        act(tmp, tmp, AF.Exp, [btmp], [btmp], scale=-1.0)
        stt(fmT[:, fct, t0:t0 + n], cps[:, 0:n], convb[:, ch:ch + 1], tmp, ALU.add, ALU.mult, [bcps, b_pfm, btmp],
            [b_fmT[fct][x] for x in tl])
        yield

    def conv_pipe(facts):
        n = len(facts)
        g = [None] * n

        def stepA(k):
            g[k] = facts[k]()
            next(g[k])

        stepA(0)
        yield
        if n > 1:
            stepA(1)
            yield
        for k in range(n):
            next(g[k])
            if k + 2 < n:
                stepA(k + 2)
            yield

    def conv_facts(wview, bw, wc0, ch, fct, tiles_n, gb):
        ntok = tiles_n * 128
        out = []
        t0 = 0
        dslot = dgc[0] % 2
        dgc[0] += 1
        while t0 < ntok:
            n = min(512, ntok - t0)
            out.append(lambda t0=t0, n=n: conv_sub(wview, bw, wc0, ch, fct, t0, n, dslot, gb))
            t0 += n
        return out

    def ssd_tile(i, g, full, wz=None, bwz=None):
        par = i % 2
        fmT, b_fmT = fmT_[g % 2], b_fmT_[g % 2]
        xsB, b_xsB = xsB_[par], b_xsB_[par]
        sz, b_sz = sz_[par], b_sz_[par]
        mCB, b_mCB = mCB_[par], b_mCB_[par]
        Rt, b_R = Rt_[par], b_R_[par]
        Et, b_E = Et_[par], b_E_[par]
        MT, b_MT = MT_[par], b_MT_[par]
        xdt, b_xdt = xdt_[par], b_xdt_[par]
        xdte, b_xdte = xdte_[par], b_xdte_[par]
        Sb, b_Sb = Sb_[par], b_Sb_[par]
        t1, b_t1 = t1_[par], b_t1_[par]
        t2, b_t2 = t2_[par], b_t2_[par]
        yfin, b_yfin = yfin_[par], b_yfin_[par]
        gn, b_gn = gn_[par], b_gn_[par]
        Yp, b_Yp = YYs[par]
        tok = slice(i * 128, (i + 1) * 128)
        d = dtall[:, i, :]
        bd = b_dt[i]
        hs = slice(4 * g, 4 * g + 4)
        if full:
            ip, bip = next_ip()
            for kc in range(16):
                mm(ip[:, 0:256], hnT[:, kc, tok], wz[:, kc, :], kc == 0, kc == 15, [b_hnT[i], bwz], [bip], inc=(kc == 15))
            silu_mul(sz, ip[:, 0:256], sz, [bip], b_sz, [b_sz])
            yield
        for j in range(3):
            tp(pT[:, j * 128:(j + 1) * 128], fmT[:, j, tok], identb, [b_fmT[j][i], b_identb], [b_pT], inc=(j == 2))
        cp(xsB, pT[:, 0:384], [b_pT], [b_xsB])
        yield
        xs4 = xsB[:, 0:256].rearrange("p (a b) -> p a b", a=4)
        tt(xdte.rearrange("p (a b) -> p a b", a=4), xs4, bc(d[:, 192 + 4 * g:196 + 4 * g], 64), ALU.mult, [b_xsB, bd], [b_xdte])
        if full:
            cb, bcb = next_ip()
            mm(cb[:, 0:128], fmT[:, 2, tok], fmT[:, 3, tok], True, True, [b_fmT[2][i], b_fmT[3][i]], [bcb])
            tt(mCB[0:64, :], cb[0:64, 0:64], tri[0:64, :], ALU.mult, [bcb, b_cst], [b_mCB])
            tt(mCB[64:128, :], cb[64:128, 64:128], tri[64:128, :], ALU.mult, [bcb, b_cst], [b_mCB])
            tt(Rt.rearrange("p (a b) -> p a b", a=4), bc1(T2, 4), bc(d[:, 32 + 4 * g:36 + 4 * g], 64), ALU.mult, [b_cst, bd], [b_R])
            yield
            mm(cb[:, 128:384], Ubd, Rt, True, True, [b_cst, b_R], [bcb])
            act(Et, cb[:, 128:384], AF.Exp, [bcb], [b_E])
            tt(MT, Et.rearrange("p (a b) -> p a b", a=4), bc1(mCB, 4), ALU.mult, [b_E, b_mCB], [b_MT])
            tt(xdt.rearrange("p (a b) -> p a b", a=4), xs4, bc(d[:, hs], 64), ALU.mult, [b_xsB, bd], [b_xdt])
            yield
            for c in range(2):
                pr = slice(64 * c, 64 * c + 64)
                for h in range(4):
                    mm(Yp[pr, 64 * h:64 * h + 64], MT[pr, h, :], xdt[pr, 64 * h:64 * h + 64], True, True,
                       [b_MT, b_xdt], [b_Yp], inc=(c == 1 and h == 3))
            yield
        for c in range(2):
            pr = slice(64 * c, 64 * c + 64)
            sn, bsn = next_ip()
            mm(sn[:, 0:256], xsB[pr, 256:384], xdte[pr, :], True, True, [b_xsB, b_xdte], [bsn])
            if full:
                act(Sb, state[:, g, :], AF.Copy, [b_state[g]], [b_Sb])
                mm(Yp[pr, 256:512], fmT[:, 3, i * 128 + 64 * c:i * 128 + 64 * c + 64], Sb, True, True,
                   [b_fmT[3][i], b_Sb], [b_Yp])
            st4 = state[:, g, :].rearrange("p (a b) -> p a b", a=4)
            tt(st4, st4, bc(d[:, 128 + 32 * c + 4 * g:132 + 32 * c + 4 * g], 64), ALU.mult, [b_state[g], bd], [b_state[g]])
            tt(state[:, g, :], state[:, g, :], sn[:, 0:256], ALU.add, [b_state[g], bsn], [b_state[g]])
            yield
        if not full:
            return
        t14 = t1.rearrange("p (a b) -> p a b", a=4)
        tt(t2.rearrange("p (a b) -> p a b", a=4), xs4, bc(dskip_r[:, hs], 64), ALU.mult, [b_xsB, b_prow], [b_t2])
        tt(t14, Yp[:, 256:512].rearrange("p (a b) -> p a b", a=4), bc(d[:, 64 + 4 * g:68 + 4 * g], 64), ALU.mult, [b_Yp, bd], [b_t1])
        tt(t1, t1, Yp[:, 0:256], ALU.add, [b_t1, b_Yp], [b_t1])
        yield
        tt(t1, t1, t2, ALU.add, [b_t1, b_t2], [b_t1])
        tt(t1, t1, sz, ALU.mult, [b_t1, b_sz], [b_t1])
        act(t2, t1, AF.Square, [b_t1], [b_t2, b_gn], accum=gn[:, 0:1])
        yield
        rstd_of(gn[:, 3:4], gn[:, 2:3], gn[:, 0:1], 256, b_gn)
        ts(yfin, t1, gn[:, 3:4], None, ALU.mult, None, [b_t1, b_gn], [b_yfin])
        yield
        for j in range(2):
            tp(pT[:, j * 128:(j + 1) * 128], yfin[:, j * 128:(j + 1) * 128], identb, [b_yfin, b_identb], [b_pT], inc=(j == 1))
        tt(mixT[:, 2 * g:2 * g + 2, tok], pT[:, 0:256].rearrange("p (a b) -> p a b", a=2),
           bc(ssdnw[:, 2 * g:2 * g + 2], 128), ALU.mult, [b_pT, b_pfm], [b_mixT[i]])
        yield

    def prefix_factors(n):
        nc2 = 2 * n
        S.op("dve", lambda e: e.memset(Wlog[:, nc2 - 1, :], 0.0), [], [b_pfx])
        for c in range(nc2 - 2, -1, -1):
            tt(Wlog[:, c, :], Wlog[:, c + 1, :], lsum[:, c + 1, :], ALU.add, [b_pfx], [b_pfx])
        tt(Dtot, Wlog[:, 0, :], lsum[:, 0, :], ALU.add, [b_pfx], [b_pfx])
        act(Wfac[:, 0:nc2, :], Wlog[:, 0:nc2, :], AF.Exp, [b_pfx], [b_pfx])
        act(Dtot, Dtot, AF.Exp, [b_pfx], [b_pfx])
        for i in range(n):
            for cc in range(2):
                pr = slice(64 * cc, 64 * cc + 64)
                tt(Ffac[pr, i, :], dtall[pr, i, 192:224], Wfac[pr, 2 * i + cc, :], ALU.mult, [b_dt[i], b_pfx], [b_pfx])

    def ssd_prefix_group(g, n):
        fmT, b_fmT = fmT_[g % 2], b_fmT_[g % 2]
        for i0 in range(0, n, 2):
            m = min(2, n - i0)
            for ii in range(m):
                for j in range(3):
                    col = (ii * 3 + j) * 128
                    tp(pT[:, col:col + 128], fmT[:, j, (i0 + ii) * 128:(i0 + ii + 1) * 128], identb,
                       [b_fmT[j][i0 + ii], b_identb], [b_pT], inc=(ii == m - 1 and j == 2))
            cp(xsB_all[:, i0:i0 + m, :], pT[:, 0:m * 384].rearrange("p (a b) -> p a b", a=m), [b_pT], [b_xsBall])
            yield
        tt(xdW_all[:, 0:n, :].rearrange("p n (a b) -> p n a b", a=4),
           xsB_all[:, 0:n, 0:256].rearrange("p n (a b) -> p n a b", a=4),
           Ffac[:, 0:n, 4 * g:4 * g + 4].unsqueeze(3).to_broadcast([128, n, 4, 64]), ALU.mult, [b_xsBall, b_pfx], [b_xdW])
        sn, bsn = next_ip()
        for i in range(n):
            mm(sn[:, 0:256], xsB_all[:, i, 256:384], xdW_all[:, i, :], i == 0, i == n - 1, [b_xsBall, b_xdW], [bsn], inc=(i == n - 1))
        st4 = state[:, g, :].rearrange("p (a b) -> p a b", a=4)
        tt(st4, st4, bc(Dtot[:, 4 * g:4 * g + 4], 64), ALU.mult, [b_state[g], b_pfx], [b_state[g]])
        tt(state[:, g, :], state[:, g, :], sn[:, 0:256], ALU.add, [b_state[g], bsn], [b_state[g]])
        yield

    def lagged(factories, lag=2):
        active = []
        idx = 0
        while idx < len(factories) or active:
            if idx < len(factories) and len(active) < 2 and (not active or active[-1][1] >= lag):
                active.append([factories[idx](), 0])
                idx += 1
            for a in list(active):
                try:
                    next(a[0])
                    a[1] += 1
                except StopIteration:
                    active.remove(a)
            yield

    def attn_tile(i, t, kv, kv_only):
        tok = slice(i * 128, (i + 1) * 128)
        cur, prv = t % 2, (t + 1) % 2
        ip, bip = next_ip()
        if not kv_only:
            for kc in range(16):
                mm(ip[:, 0:256], hnT[:, kc, tok], aq[:, kc, :], kc == 0, kc == 15, [b_hnT[i], b_aq], [bip], inc=False)
        for kc in range(16):
            mm(ip[:, 256:384], hnT[:, kc, tok], akv[:, kc, :], kc == 0, kc == 15, [b_hnT[i], b_akv], [bip], inc=(kc == 15))
        if kv_only:
            act(qkf[:, 256:320], ip[:, 256:320], AF.Copy, [bip], [b_qkf])
        else:
            act(qkf, ip[:, 0:320], AF.Copy, [bip], [b_qkf])
        act(vaug[kv][cur][:, 0:64], ip[:, 320:384], AF.Copy, [bip], [b_vaug[kv][cur]])
        yield
        if not kv_only:
            ip2, bip2 = next_ip()
            for kc in range(16):
                mm(ip2[:, 0:256], hnT[:, kc, tok], ag[:, kc, :], kc == 0, kc == 15, [b_hnT[i], b_ag], [bip2], inc=(kc == 15))
            silu_mul(sg, ip2[:, 0:256], sg, [bip2], b_sg, [b_sg])
            yield
        q4 = qkf.rearrange("p (h two f) -> p h two f", h=5, two=2)
        o4 = qkb.rearrange("p (h two f) -> p h two f", h=5, two=2)
        hsl = slice(4, 5) if kv_only else slice(0, 5)
        nh = 1 if kv_only else 5
        q4 = q4[:, hsl]
        o4 = o4[:, hsl]
        q1, q2 = q4[:, :, 0, :], q4[:, :, 1, :]
        cosb = bc1(tokb[:, i, 2:34], nh)
        sinb = bc1(tokb[:, i, 34:66], nh)
        ra3 = ra.rearrange("p (h f) -> p h f", h=5)[:, hsl]
        rb3 = rb.rearrange("p (h f) -> p h f", h=5)[:, hsl]
        tt(ra3, q1, cosb, ALU.mult, [b_qkf, b_tokb], [b_ra])
        tt(rb3, q2, sinb, ALU.mult, [b_qkf, b_tokb], [b_rb])
        tt(o4[:, :, 0, :], ra3, rb3, ALU.subtract, [b_ra, b_rb], [b_qkb])
        tt(ra3, q2, cosb, ALU.mult, [b_qkf, b_tokb], [b_ra])
        tt(rb3, q1, sinb, ALU.mult, [b_qkf, b_tokb], [b_rb])
        tt(o4[:, :, 1, :], ra3, rb3, ALU.add, [b_ra, b_rb], [b_qkb])
        js = [4] if kv_only else [0, 1, 2, 3, 4]
        for j in js:
            tp(pT[0:64, j * 128:(j + 1) * 128], qkb[:, j * 64:(j + 1) * 64], identb, [b_qkb, b_identb], [b_pT], inc=(j == 4))
        act(kT[kv][cur][0:64, :], pT[0:64, 512:640], AF.Copy, [b_pT], [b_kT[kv][cur]])
        if kv_only:
            yield
            return
        act(qT[0:64, :, :], pT[0:64, 0:512].rearrange("p (a b) -> p a b", a=4), AF.Copy, [b_pT], [b_qT])
        yield
        for c in range(2):
            pr = slice(64 * c, 64 * c + 64)
            if c == 0:
                kA, vA, bkA, bvA, tA = kT[kv][prv], vaug[kv][prv], b_kT[kv][prv], b_vaug[kv][prv], t - 1
                kB, vB, bkB, bvB, tB = kT[kv][cur], vaug[kv][cur], b_kT[kv][cur], b_vaug[kv][cur], t
                pB = slice(0, 64)
            else:
                kA, vA, bkA, bvA, tA = kT[kv][cur], vaug[kv][cur], b_kT[kv][cur], b_vaug[kv][cur], t
                kB, vB, bkB, bvB, tB = kT[kv][prv], vaug[kv][prv], b_kT[kv][prv], b_vaug[kv][prv], t - 1
                pB = slice(64, 128)
            qc = qT[0:64, :, 64 * c:64 * c + 64]
            mm(ST[:, 0:256], kA[0:64, :], qc, True, True, [bkA, b_qT], [b_ST], inc=False)
            mm(ST[pB, 256:512], kB[0:64, pB], qc, True, True, [bkB, b_qT], [b_ST])
            act(PT[:, 0:256], ST[:, 0:256], AF.Exp, [b_ST, b_kb], [b_PT], bias=kb[:, tA:tA + 1], scale=0.125)
            act(PT[pB, 256:512], ST[pB, 256:512], AF.Exp, [b_ST, b_kb], [b_PT], bias=kb[pB, tB:tB + 1], scale=0.125)
            yield
            for r in range(4):
                mm(PV[pr, r * 65:(r + 1) * 65], PT[:, r * 64:(r + 1) * 64], vA[:, :], True, False, [b_PT, bvA], [b_PV], inc=False)
                mm(PV[pr, r * 65:(r + 1) * 65], PT[pB, 256 + r * 64:256 + (r + 1) * 64], vB[pB, :], False, True,
                   [b_PT, bvB], [b_PV], inc=(r == 3))
            yield
        pv4 = PV[:, 0:260].rearrange("p (a b) -> p a b", a=4)
        tt(den[:, 0:4], pv4[:, :, 64], esink_r[:, 4 * kv:4 * kv + 4], ALU.add, [b_PV, b_prow], [b_den])
        recip(den[:, 0:4], den[:, 0:4], [b_den], [b_den])
        tt(att.rearrange("p (a b) -> p a b", a=4), pv4[:, :, 0:64], bc(den[:, 0:4], 64), ALU.mult, [b_PV, b_den], [b_att])
        tt(attb, att, sg, ALU.mult, [b_att, b_sg], [b_attb])
        for j in range(2):
            tp(pT[:, j * 128:(j + 1) * 128], attb[:, j * 128:(j + 1) * 128], identb, [b_attb, b_identb], [b_pT], inc=(j == 1))
        act(mixT[:, 16 + 2 * kv:18 + 2 * kv, tok], pT[:, 0:256].rearrange("p (a b) -> p a b", a=2), AF.Copy, [b_pT], [b_mixT[i]])
        yield

    obanks = [(CBs, b_CBs), (YY, b_YY), (SN, b_SN), (ST, b_ST)]

    def outproj_tile(i, t, orow):
        tok = slice(i * 128, (i + 1) * 128)
        s = xc[0] % NX
        xc[0] += 1
        xb_, bx = xbuf[s], b_xbuf[s]
        st, bst = stt_[s], b_stt[s]
        dma("sp", xb_, xin[t * 128:(t + 1) * 128, :], bx, [], [bx])
        for dc in range(4):
            ob, bob = obanks[dc]
            for m in range(24):
                mm(ob[:, 0:512], mixT[:, m, tok], wout[:, m, dc * 512:(dc + 1) * 512], m == 0, m == 23,
                   [b_mixT[i], b_wout[dc]], [bob], inc=(m == 23))
            act(junk, ob[:, 0:512], AF.Square, [bob], [b_junk, bst], accum=st[:, 4 + dc:5 + dc])
        S.op("dve", lambda e: e.reduce_sum(out=st[:, 0:1], in_=st[:, 4:8], axis=mybir.AxisListType.X), [bst], [bst])
        rstd_of(st[:, 3:4], st[:, 2:3], st[:, 0:1], D, bst)
        for dc in range(4):
            ob, bob = obanks[dc]
            cs = slice(dc * 512, (dc + 1) * 512)
            stt(otmp, ob[:, 0:512], st[:, 3:4], nwpost[:, cs], ALU.mult, ALU.mult, [bob, bst, b_prow], [b_otmp])
            tt(xb_[:, cs], xb_[:, cs], otmp, ALU.add, [bx, b_otmp], [bx])
        dma("sp", out_d[orow * 128:(orow + 1) * 128, :], xb_, bx, [bx], [])

    def interleave_gen(chains):
        live = [[g, w, 0] for g, w in chains]
        while live:
            live.sort(key=lambda c: c[2] / c[1])
            c = live[0]
            try:
                next(c[0])
                c[2] += 1
                yield
            except StopIteration:
                live.remove(c)

    def interleave(chains, key=None):
        live = [[g, w, 0, j] for j, (g, w) in enumerate(chains)]
        if step_counts is not None and key in step_counts:
            for c in live:
                c[1] = max(1, step_counts[key][c[3]])
        done = {}
        while live:
            live.sort(key=lambda c: c[2] / c[1])
            c = live[0]
            try:
                next(c[0])
                c[2] += 1
            except StopIteration:
                done[c[3]] = c[2]
                live.remove(c)
        counts_out[key] = [done[j] for j in range(len(chains))]

    def run_block(tiles, own, last_prefix, own_row0):
        n = len(tiles)
        t0 = tiles[0]
        dma("sp", tokb[:, 0:n, :], tokc[t0 * 128:(t0 + n) * 128, :].rearrange("(n p) c -> p n c", p=128), b_tokb, [], [b_tokb])
        dma("pool", wdt, w_in_v[:, :, OFF_DT:OFF_DT + 32], b_wdt, [], [b_wdt])
        for i, t in enumerate(tiles):
            stage0(i, t)
            if i >= 1:
                stage1(i - 1, not own)
        stage1(n - 1, not own)
        if not own and os.environ.get("K_NOPF", "0") != "1":
            prefix_factors(n)
        needC = own or last_prefix
        ippool[0] = 3 if (own or os.environ.get("K_POOL3", "0") == "1") else 7

        wzs = {}
        nsub_ = (n * 128 + 511) // 512
        w_conv = (4 if needC else 3) * nsub_ + 1
        w_ssd = (5 * n + 5) if own else ((n + 1) // 2 + 1)

        def group_conv(g):
            gb = g % 2
            wx, bwx = load_w([(OFF_XS + g * 256, 256)])
            pcs = [(OFF_B + g * 128, 128)] + ([(OFF_C + g * 128, 128)] if needC else [])
            wbc, bwbc = load_w(pcs)
            if own:
                wzs[g] = load_w([(OFF_Z + g * 256, 256)])
            facts = conv_facts(wx, bwx, 0, 2 * g, 0, n, gb) + conv_facts(wx, bwx, 128, 2 * g + 1, 1, n, gb)
            facts += conv_facts(wbc, bwbc, 0, 16 + g, 2, n, gb)
            if needC:
                facts += conv_facts(wbc, bwbc, 128, 24 + g, 3, n, gb)
            yield from conv_pipe(facts)

        def group_ssd(g):
            if own:
                wz, bwz = wzs[g]
                yield from lagged([(lambda i=i: ssd_tile(i, g, True, wz, bwz)) for i in range(n)])
            else:
                yield from ssd_prefix_group(g, n)

        def chainS():
            yield from group_conv(0)
            for g in range(8):
                subs = [(group_ssd(g), w_ssd)]
                if g < 7:
                    subs.append((group_conv(g + 1), w_conv))
                yield from interleave_gen(subs)

        def load_att(kv, kv_only):
            if not kv_only:
                dma("pool", aq, w_in_v[:, :, OFF_Q + kv * 256:OFF_Q + kv * 256 + 256], b_aq, [], [b_aq])
                dma("pool", ag, w_in_v[:, :, OFF_G + kv * 256:OFF_G + kv * 256 + 256], b_ag, [], [b_ag])
            dma("pool", akv[:, :, 0:64], w_in_v[:, :, OFF_K + kv * 64:OFF_K + kv * 64 + 64], b_akv, [], [b_akv])
            dma("pool", akv[:, :, 64:128], w_in_v[:, :, OFF_V + kv * 64:OFF_V + kv * 64 + 64], b_akv, [], [b_akv])

        def chainA():
            for kv in range(4):
                load_att(kv, False)
                for i, t in enumerate(tiles):
                    yield from attn_tile(i, t, kv, False)

        nsub = (n * 128 + 511) // 512
        if own:
            wS = w_conv + 8 * (w_ssd + w_conv)
            wA = 4 * n * 9
            interleave([(chainS(), wS), (chainA(), wA)], key=("own", tiles[0]))
            S.barrier()
            for dc in range(4):
                dma("pool", wout[:, :, dc * 512:(dc + 1) * 512], w_out_v[:, :, dc * 512:(dc + 1) * 512], b_wout[dc], [], [b_wout[dc]])
            for i, t in enumerate(tiles):
                outproj_tile(i, t, own_row0 + i)
            S.barrier()
        else:
            interleave([(chainS(), 1)], key=("pre", tiles[0]))
            if last_prefix:
                for kv in range(4):
                    load_att(kv, True)
                    for _ in attn_tile(n - 1, tiles[-1], kv, True):
                        pass

    t = 0
    for bi, nb in enumerate(pre_blocks):
        run_block(list(range(t, t + nb)), False, bi == len(pre_blocks) - 1, 0)
        t += nb
    if os.environ.get("K_NOBAR", "0") != "1":
        S.barrier()
    row0 = 0
    for nb in own_blocks:
        run_block(list(range(t, t + nb)), True, False, row0)
        t += nb
        row0 += nb
    S.final_wait("sp", b_xbuf)
    if step_counts is None:
        es.close()
        return counts_out
    S.emit(nc)
    es.close()
    return nc


def make_consts():
    k = np.arange(128)
    same = (k[:, None] // 64) == (k[None, :] // 64)
    c = np.zeros((128, 768), np.float32)
    c[:, 0:128] = np.eye(128)
    c[:, 128:256] = (same & (k[:, None] > k[None, :]))
    c[:, 256:384] = (same & (k[:, None] <= k[None, :]))
    l = np.arange(64)
    c[:, 384:448] = ((k[:, None] % 64) <= l[None, :])
    c[:, 448:512] = (l[None, :] >= (k[:, None] % 64))
    c[:, 512:640] = (k[:, None] < 64)
    c[:, 640:768] = (k[:, None] >= 64)
    return c


def kernel(x, meta_tokens, norm_pre_w, w_in, conv_w, conv_b, dt_bias, a_log, d_skip,
           ssd_norm_w, attn_sinks, w_out, norm_post_w, _O=None, _pre_max=6, _own_max=6, _same_sync=("dve", "act")):
    x = np.asarray(x, np.float32)
    bsz, seq, _ = x.shape
    nchunk = (PADL + NMETA + seq) // 64
    G = (nchunk + 1) // 2
    O = (G + 1) // 2 if _O is None else _O
    assert 2 * O - 1 >= G
    P = O - 1
    NT = P + O
    f32 = np.float32
    nslot = (2 * O - 1) * 128
    idx = np.arange(nslot)
    valid = ((idx >= PADL) & (idx < PADL + NMETA + seq)).astype(f32)
    pos = (idx - PADL).astype(f32)
    inv = (10000.0 ** (-np.arange(32, dtype=f32) / f32(32))).astype(f32)
    ang = (pos[:, None] * inv[None, :]).astype(f32)
    tokg = np.zeros((nslot, 66), f32)
    tokg[:, 0] = valid
    tokg[:, 1] = np.where(valid > 0, 0.0, NEG)
    tokg[:, 2:34] = np.cos(ang)
    tokg[:, 34:66] = np.sin(ang)
    meta = np.asarray(meta_tokens, f32)
    cst = make_consts()
    pfm = np.zeros((128, 192), f32)
    pfm[:, 0:16] = np.asarray(norm_pre_w, f32).reshape(16, 128).T
    pfm[:, 16:32] = np.asarray(ssd_norm_w, f32).reshape(16, 128).T
    cw = np.asarray(conv_w, f32).reshape(4, 32, 128)
    pfm[:, 32:160] = cw.transpose(2, 1, 0).reshape(128, 128)
    pfm[:, 160:192] = np.asarray(conv_b, f32).reshape(32, 128).T
    prow = np.concatenate([np.asarray(dt_bias, f32).reshape(-1), np.asarray(a_log, f32).reshape(-1),
                           np.asarray(d_skip, f32).reshape(-1), np.asarray(attn_sinks, f32).reshape(-1),
                           np.asarray(norm_post_w, f32).reshape(-1)]).reshape(1, 2160).astype(f32)
    w_in2 = np.ascontiguousarray(np.asarray(w_in, f32).reshape(D, DIN))
    w_out2 = np.ascontiguousarray(np.asarray(w_out, f32).reshape(DMIX, D))

    in_maps = []
    own_start = []
    for b in range(bsz):
        gx = np.zeros((nslot, D), f32)
        gx[PADL:PADL + NMETA] = meta
        gx[PADL + NMETA:PADL + NMETA + seq] = x[b]
        for j in range(2):
            if j == 0:
                xin = np.concatenate([np.zeros((P * 128, D), f32), gx[:O * 128]], 0)
                tk = np.zeros((NT * 128, 66), f32)
                tk[:P * 128, 1] = NEG
                tk[P * 128:] = tokg[:O * 128]
                own_start.append(0)
            else:
                xin = gx
                tk = tokg
                own_start.append(P * 128)
            kball = np.ascontiguousarray(tk[:, 1].reshape(NT, 128).T)
            in_maps.append({"xin": np.ascontiguousarray(xin), "tokc": np.ascontiguousarray(tk), "kball": kball,
                            "w_in": w_in2, "w_out": w_out2, "cst": cst, "pfm": pfm, "prow": prow})
    ncores = len(in_maps)
    counts = build_program(O, _pre_max, _own_max, _same_sync, None)
    nc = build_program(O, _pre_max, _own_max, _same_sync, counts)
    res = run_bass_kernel_spmd(nc, in_maps, core_ids=list(range(ncores)))
    out = np.zeros((bsz, seq, D), f32)
    for ci in range(ncores):
        b = ci // 2
        r = np.asarray(res.results[ci]["out"])
        s0 = own_start[ci]
        n0 = s0 - (PADL + NMETA)
        lo = max(0, -n0)
        hi = min(O * 128, seq - n0)
        out[b, n0 + lo:n0 + hi] = r[lo:hi]
    return out
```
